# Optimizing a Trainium2 kernel written in Bass

```python
import jax, jax.numpy as jnp
from jax import lax
import numpy as np

D_MODEL = 1024
BATCH = 8
SEQ = 2048
DEPTH = 4

CHUNK = 64
Q_BLOCK = 128
EPS = 1e-6

BRANCH_WIDTH = D_MODEL // 2
N_BRANCH = 4

A_HEADS = 4
A_DK = (BRANCH_WIDTH // 2) // A_HEADS
A_DV = BRANCH_WIDTH // A_HEADS
A_RANK = 16
A_GATE_NORM = 16.0

B_HEADS = 8
B_DH = BRANCH_WIDTH // B_HEADS
B_PREV_CHUNKS = 8
B_MAX_REL = 128

C_WIDTH = BRANCH_WIDTH
C_BLOCKS = 8
C_BLOCK_DIM = C_WIDTH // C_BLOCKS
C_CONV = 4
C_POW = 8.0

D_HEADS = 8
D_DH = BRANCH_WIDTH // D_HEADS

FFN_HIDDEN = -(-8 * D_MODEL // (3 * 256)) * 256

IN_SIZES = (
    A_HEADS * A_DK, A_HEADS * A_DK, A_HEADS * A_DV, A_RANK, A_HEADS * A_DV,
    B_HEADS * B_DH, B_HEADS * B_DH, B_HEADS * B_DH,
    C_WIDTH, C_WIDTH,
    D_HEADS * D_DH, D_HEADS * D_DH, D_HEADS * D_DH,
    N_BRANCH * D_MODEL,
)
IN_TOTAL = int(sum(IN_SIZES))
SPLIT_POINTS = tuple(int(v) for v in np.cumsum(IN_SIZES)[:-1])

kernel_name = 'hybrid_chunk_causal_parallel_mixer_trunk'

F32 = jnp.float32


def rmsnorm(x, g):
    xf = x.astype(F32)
    y = xf * lax.rsqrt(jnp.mean(xf * xf, axis=-1, keepdims=True) + EPS) * g.astype(F32)
    return y.astype(x.dtype)


def gla_mixer(q, k, v, r, g, w_gk, b_gk, norm_g):
    bsz, s, _ = q.shape
    nc = s // CHUNK
    gk = jax.nn.log_sigmoid((r @ w_gk + b_gk).astype(F32)) / A_GATE_NORM

    def chunks(t, d):
        return t.astype(F32).reshape(bsz, nc, CHUNK, A_HEADS, d).transpose(1, 0, 2, 3, 4)

    qc = chunks(q, A_DK) * (A_DK ** -0.5)
    kc = chunks(k, A_DK)
    vc = chunks(v, A_DV)
    cum = jnp.cumsum(chunks(gk, A_DK), axis=2)
    tot = cum[:, :, -1]
    k_dec = kc * jnp.exp(tot[:, :, None] - cum)

    def step(state, xs):
        q_c, kd_c, v_c, tot_c = xs
        state = jnp.exp(tot_c)[..., None] * state + jnp.einsum('bthk,bthv->bhkv', kd_c, v_c)
        return state, jnp.einsum('bthk,bhkv->bthv', q_c, state)

    state0 = jnp.zeros((bsz, A_HEADS, A_DK, A_DV), F32)
    _, o = lax.scan(step, state0, (qc, k_dec, vc, tot))
    o = o.transpose(1, 0, 2, 3, 4).reshape(bsz, s, A_HEADS, A_DV)
    o = rmsnorm(o, norm_g).reshape(bsz, s, A_HEADS * A_DV)
    return (o * jax.nn.silu(g.astype(F32))).astype(q.dtype)


def chunk_rel_attention(q, k, v, rel_table):
    bsz, s, _ = q.shape
    nc = s // CHUNK
    n_band = B_PREV_CHUNKS + 1
    band = n_band * CHUNK
    qc = q.reshape(bsz, nc, CHUNK, B_HEADS, B_DH)
    idx = jnp.arange(nc)[:, None] + jnp.arange(n_band)[None, :]

    def gather_band(t):
        tc = t.reshape(bsz, nc, CHUNK, B_HEADS, B_DH)
        tp = jnp.pad(tc, ((0, 0), (B_PREV_CHUNKS, 0), (0, 0), (0, 0), (0, 0)))
        return tp[:, idx].reshape(bsz, nc, band, B_HEADS, B_DH)

    kb = gather_band(k)
    vb = gather_band(v)
    sc = jnp.einsum('bnqhd,bnkhd->bhnqk', qc, kb).astype(F32) * (B_DH ** -0.5)
    qi = jnp.arange(CHUNK)[:, None]
    kj = jnp.arange(band)[None, :]
    rel = jnp.clip(B_PREV_CHUNKS * CHUNK + qi - kj, -B_MAX_REL, B_MAX_REL) + B_MAX_REL
    bias = rel_table.astype(F32)[:, rel]
    valid = (jnp.arange(nc)[:, None] - B_PREV_CHUNKS) * CHUNK + jnp.arange(band)[None, :] >= 0
    sc = jnp.where(valid[None, None, :, None, :], sc + bias[None, :, None], -1e30)
    p = jax.nn.softmax(sc, axis=-1)
    o = jnp.einsum('bhnqk,bnkhd->bnqhd', p.astype(v.dtype), vb)
    return o.reshape(bsz, s, B_HEADS * B_DH)


def rglru_mixer(gate_in, x_in, conv_w, conv_b, w_a, b_a, w_x, b_x, lam):
    bsz, s, _ = x_in.shape
    xp = jnp.pad(x_in, ((0, 0), (C_CONV - 1, 0), (0, 0)))
    xc = conv_b
    for j in range(C_CONV):
        xc = xc + xp[:, j:j + s] * conv_w[j]
    xg = xc.reshape(bsz, s, C_BLOCKS, C_BLOCK_DIM)
    r = jax.nn.sigmoid(jnp.einsum('bsgi,gij->bsgj', xg, w_a).reshape(bsz, s, C_WIDTH) + b_a)
    i = jax.nn.sigmoid(jnp.einsum('bsgi,gij->bsgj', xg, w_x).reshape(bsz, s, C_WIDTH) + b_x)
    log_a = -C_POW * r.astype(F32) * jax.nn.softplus(-lam.astype(F32))
    a = jnp.exp(log_a)
    bx = jnp.sqrt(-jnp.expm1(2.0 * log_a)) * (i * xc).astype(F32)

    def combine(left, right):
        a1, b1 = left
        a2, b2 = right
        return a1 * a2, a2 * b1 + b2

    _, h = lax.associative_scan(combine, (a, bx), axis=1)
    return jax.nn.gelu(gate_in) * h.astype(x_in.dtype)


def stick_breaking_attention(q, k, v):
    bsz, s, _ = q.shape
    qh = q.reshape(bsz, s, D_HEADS, D_DH)
    kh = k.reshape(bsz, s, D_HEADS, D_DH)
    vh = v.reshape(bsz, s, D_HEADS, D_DH)
    outs = []
    for blk in range(s // Q_BLOCK):
        end = (blk + 1) * Q_BLOCK
        z = jnp.einsum('bqhd,bkhd->bhqk', qh[:, blk * Q_BLOCK:end], kh[:, :end]).astype(F32) * (D_DH ** -0.5)
        qpos = blk * Q_BLOCK + jnp.arange(Q_BLOCK)[:, None]
        kpos = jnp.arange(end)[None, :]
        before = kpos < qpos
        log1m = jnp.where(before, jax.nn.log_sigmoid(-z), 0.0)
        tail = lax.cumsum(log1m, axis=3, reverse=True) - log1m
        att = jnp.where(before, jnp.exp(jax.nn.log_sigmoid(z) + tail), 0.0)
        outs.append(jnp.einsum('bhqk,bkhd->bqhd', att.astype(v.dtype), vh[:, :end]))
    return jnp.concatenate(outs, axis=1).reshape(bsz, s, D_HEADS * D_DH)


def setup_inputs(seed: int = 0) -> dict:
    key = jax.random.key(seed)
    ks = jax.random.split(key, 24)
    L = DEPTH

    def nrm(k, shape, scale):
        return jax.random.normal(k, shape, F32) * scale

    a0 = jax.random.uniform(ks[13], (L, C_WIDTH), F32, 0.9, 0.999)
    u = a0 ** (1.0 / C_POW)
    c_lambda = jnp.log(u) - jnp.log1p(-u)
    return {
        'x': nrm(ks[0], (BATCH, SEQ, D_MODEL), 1.0),
        'norm_mix': 1.0 + nrm(ks[1], (L, D_MODEL), 0.02),
        'w_in': nrm(ks[2], (L, D_MODEL, IN_TOTAL), D_MODEL ** -0.5),
        'a_w_gk': nrm(ks[3], (L, A_RANK, A_HEADS * A_DK), A_RANK ** -0.5),
        'a_b_gk': nrm(ks[4], (L, A_HEADS * A_DK), 0.1),
        'a_norm': 1.0 + nrm(ks[5], (L, A_DV), 0.02),
        'b_rel_bias': nrm(ks[6], (L, B_HEADS, 2 * B_MAX_REL + 1), 0.1),
        'c_conv_w': nrm(ks[7], (L, C_CONV, C_WIDTH), C_CONV ** -0.5),
        'c_conv_b': nrm(ks[8], (L, C_WIDTH), 0.01),
        'c_w_a': nrm(ks[9], (L, C_BLOCKS, C_BLOCK_DIM, C_BLOCK_DIM), C_BLOCK_DIM ** -0.5),
        'c_b_a': nrm(ks[10], (L, C_WIDTH), 0.01),
        'c_w_x': nrm(ks[11], (L, C_BLOCKS, C_BLOCK_DIM, C_BLOCK_DIM), C_BLOCK_DIM ** -0.5),
        'c_b_x': nrm(ks[12], (L, C_WIDTH), 0.01),
        'c_lambda': c_lambda,
        'w_branch': nrm(ks[14], (L, N_BRANCH, BRANCH_WIDTH, D_MODEL), BRANCH_WIDTH ** -0.5),
        'w_out': nrm(ks[15], (L, D_MODEL, D_MODEL), D_MODEL ** -0.5),
        'norm_ffn': 1.0 + nrm(ks[16], (L, D_MODEL), 0.02),
        'w_ffn_gate': nrm(ks[17], (L, D_MODEL, FFN_HIDDEN), D_MODEL ** -0.5),
        'w_ffn_up': nrm(ks[18], (L, D_MODEL, FFN_HIDDEN), D_MODEL ** -0.5),
        'w_ffn_down': nrm(ks[19], (L, FFN_HIDDEN, D_MODEL), FFN_HIDDEN ** -0.5),
        'norm_final': 1.0 + nrm(ks[20], (D_MODEL,), 0.02),
    }


def reference(x, norm_mix, w_in, a_w_gk, a_b_gk, a_norm, b_rel_bias, c_conv_w, c_conv_b,
              c_w_a, c_b_a, c_w_x, c_b_x, c_lambda, w_branch, w_out, norm_ffn,
              w_ffn_gate, w_ffn_up, w_ffn_down, norm_final):
    bsz, s, _ = x.shape
    h = x
    for l in range(DEPTH):
        xn = rmsnorm(h, norm_mix[l])
        proj = xn @ w_in[l]
        (a_q, a_k, a_v, a_r, a_g, b_q, b_k, b_v, c_g, c_x,
         d_q, d_k, d_v, gate_logits) = jnp.split(proj, SPLIT_POINTS, axis=-1)

        y_a = gla_mixer(a_q, a_k, a_v, a_r, a_g, a_w_gk[l], a_b_gk[l], a_norm[l])
        y_b = chunk_rel_attention(b_q, b_k, b_v, b_rel_bias[l])
        y_c = rglru_mixer(c_g, c_x, c_conv_w[l], c_conv_b[l], c_w_a[l], c_b_a[l],
                          c_w_x[l], c_b_x[l], c_lambda[l])
        y_d = stick_breaking_attention(d_q, d_k, d_v)

        branches = jnp.stack([y_a, y_b, y_c, y_d], axis=2)
        widened = jnp.einsum('bsnw,nwd->bsnd', branches, w_branch[l])
        gates = jax.nn.sigmoid(gate_logits).reshape(bsz, s, N_BRANCH, D_MODEL)
        mixed = jnp.sum(gates * widened, axis=2)
        h = h + mixed @ w_out[l]

        hn = rmsnorm(h, norm_ffn[l])
        h = h + (jax.nn.silu(hn @ w_ffn_gate[l]) * (hn @ w_ffn_up[l])) @ w_ffn_down[l]
    return rmsnorm(h, norm_final)
```

```python
import types
import numpy as np
from contextlib import ExitStack
import concourse.bass as bass
import concourse.mybir as mybir
from concourse.bass_utils import run_bass_kernel_spmd

F32 = mybir.dt.float32
BF16 = mybir.dt.bfloat16
AF = mybir.ActivationFunctionType
ALU = mybir.AluOpType
AX = mybir.AxisListType

D_MODEL = 1024
SEQ = 2048
DEPTH = 4
NCORES = 8
EPS = 1e-6
FFN_HIDDEN = 2816
IN_SIZES = (256, 256, 512, 16, 512, 512, 512, 512, 512, 512, 512, 512, 512, 4096)
IN_OFF = [int(v) for v in np.cumsum((0,) + IN_SIZES)]
(O_AQ, O_AK, O_AV, O_AR, O_AG, O_BQ, O_BK, O_BV, O_CG, O_CX, O_DQ, O_DK, O_DV, O_GATE) = IN_OFF[:14]
IN_TOTAL = IN_OFF[14]


class Buf:
    __slots__ = ("name", "last_w", "readers", "dsem", "dcnt")

    def __init__(self, name):
        self.name = name
        self.last_w = None
        self.readers = []
        self.dsem = None
        self.dcnt = 0


def freeze(fn):
    if fn.__closure__ is None:
        return fn
    cells = []
    for c in fn.__closure__:
        try:
            cells.append(types.CellType(c.cell_contents))
        except ValueError:
            cells.append(c)
    return types.FunctionType(fn.__code__, fn.__globals__, fn.__name__, fn.__defaults__, tuple(cells))


class K:
    ENG = ("pe", "act", "dve", "pool", "sp")

    def __init__(self, nc, stack):
        self.nc = nc
        self.stack = stack
        self.eng = {"pe": nc.tensor, "act": nc.scalar, "dve": nc.vector, "pool": nc.gpsimd, "sp": nc.sync}
        self.sem = {}
        self.cnt = {}
        for e in self.ENG:
            self.sem[e] = stack.enter_context(nc.semaphore("s_" + e))
            self.cnt[e] = 0
        self.seen = {e: {} for e in self.ENG}
        self.pend = None
        self.sched_cache = {}
        self.prog = {e: [] for e in self.ENG}
        self.n_inst = 0
        self.n_wait = 0

    def _sem_of(self, key):
        if isinstance(key, str):
            return self.sem[key]
        return key.dsem

    def _wait(self, on, tokens):
        need = {}
        for tok in tokens:
            if tok is None:
                continue
            key, val = tok
            if on == "pe" and key == "pe":
                continue
            if self.seen[on].get(key, 0) >= val:
                continue
            if need.get(key, 0) < val:
                need[key] = val
        for key, val in need.items():
            self.prog[on].append(("w", self._sem_of(key), val))
            self.seen[on][key] = val
            self.n_wait += 1

    def deps(self, reads, writes):
        toks = []
        for b in reads:
            toks.append(b.last_w)
        for b in writes:
            toks.append(b.last_w)
            toks.extend(b.readers)
        return toks

    def commit(self, tok, reads, writes):
        for b in reads:
            b.readers.append(tok)
            if len(b.readers) > 24:
                best = {}
                for k_, v_ in b.readers:
                    if best.get(k_, 0) < v_:
                        best[k_] = v_
                b.readers = list(best.items())
        for b in writes:
            b.last_w = tok
            b.readers = []

    def op(self, on, fn, reads=(), writes=(), extra=(), n=512, tab=None):
        if self.pend is not None:
            self.pend.append(("op", on, freeze(fn), tuple(reads), tuple(writes), self.est(on, n, 1), tab))
            return None
        self._wait(on, self.deps(reads, writes) + list(extra))
        self.cnt[on] += 1
        self.prog[on].append(("i", freeze(fn), self.sem[on], 1))
        tok = (on, self.cnt[on])
        self.commit(tok, reads, writes)
        self.n_inst += 1
        return tok

    def group(self, on, fns, reads=(), writes=(), n=512):
        if self.pend is not None:
            self.pend.append(("group", on, [freeze(f) for f in fns], tuple(reads), tuple(writes), self.est(on, n, len(fns)), None))
            return None
        self._wait(on, self.deps(reads, writes))
        for fn in fns[:-1]:
            self.prog[on].append(("i", freeze(fn), None, 0))
            self.n_inst += 1
        self.n_inst += 1
        self.cnt[on] += 1
        self.prog[on].append(("i", freeze(fns[-1]), self.sem[on], 1))
        tok = (on, self.cnt[on])
        self.commit(tok, reads, writes)
        return tok

    def dma(self, on, out, in_, reads=(), writes=(), sembuf=None):
        sb = sembuf or (writes[0] if writes else reads[0])
        if self.pend is not None:
            self.pend.append(("dma", on, (out, in_, sb), tuple(reads), tuple(writes), 2.0, None))
            return None
        if sb.dsem is None:
            kind = "sw" if on == "pool" else "hw"
            free = getattr(self, "free_dsems", {}).get(kind)
            sb.name = kind + ":" + sb.name
            if free:
                sb.dsem, sb.dcnt = free.pop()
            else:
                self._nsem = getattr(self, "_nsem", 0) + 1
                sb.dsem = self.stack.enter_context(self.nc.semaphore(f"d{self._nsem}_" + sb.name))
        self._wait(on, self.deps(reads, writes))
        sb.dcnt += 16
        self.prog[on].append(("i", (lambda e, o=out, i=in_: e.dma_start(out=o, in_=i)), sb.dsem, 16))
        tok = (sb, sb.dcnt)
        self.commit(tok, reads, writes)
        self.n_inst += 1
        return tok

    @staticmethod
    def est(on, n, cnt):
        if on == "pe":
            return cnt * (0.035 + n / 2400.0)
        if on == "act":
            return cnt * (0.22 + n / 1400.0)
        if on == "dve":
            return cnt * (0.12 + n / 960.0)
        if on == "pool":
            return cnt * (0.25 + n / 900.0)
        return 2.0

    def sched_begin(self):
        assert self.pend is None
        self.pend = []

    def sched_end(self, key=None):
        pend, self.pend = self.pend, None
        preds = self._dag(pend)
        order = self.sched_cache.get(key) if key is not None else None
        if order is not None:
            ok = len(order) == len(pend)
            if ok:
                pos = [0] * len(pend)
                for j, i in enumerate(order):
                    pos[i] = j
                ok = all(pos[p] < pos[i] for i in range(len(pend)) for p in preds[i])
            if not ok:
                order = None
        if order is None:
            order = self._schedule(pend, preds)
            if key is not None:
                self.sched_cache[key] = order
        for i in order:
            kind, on, payload, reads, writes, dur, tab = pend[i]
            if kind == "op":
                self.op(on, payload, reads=reads, writes=writes)
            elif kind == "group":
                self.group(on, payload, reads=reads, writes=writes)
            else:
                out, in_, sb = payload
                self.dma(on, out, in_, reads=reads, writes=writes, sembuf=sb)

    def _dag(self, pend):
        n = len(pend)
        preds = [set() for _ in range(n)]
        last_w, readers = {}, {}
        for i, (kind, on, payload, reads, writes, dur, tab) in enumerate(pend):
            for b in reads:
                w = last_w.get(id(b))
                if w is not None:
                    preds[i].add(w)
            for b in writes:
                w = last_w.get(id(b))
                if w is not None:
                    preds[i].add(w)
                for r in readers.get(id(b), ()):
                    preds[i].add(r)
            for b in reads:
                readers.setdefault(id(b), []).append(i)
            for b in writes:
                last_w[id(b)] = i
                readers[id(b)] = []
            preds[i].discard(i)
        return preds

    def _schedule(self, pend, preds):
        import heapq
        n = len(pend)
        succs = [[] for _ in range(n)]
        for i in range(n):
            for p in preds[i]:
                succs[p].append(i)
        prio = [0.0] * n
        for i in range(n - 1, -1, -1):
            m = 0.0
            for s in succs[i]:
                if prio[s] > m:
                    m = prio[s]
            prio[i] = pend[i][5] + m
        npred = [len(preds[i]) for i in range(n)]
        ready_t = [0.0] * n
        finish = [0.0] * n
        engs = list(self.ENG)
        fut = {e: [] for e in engs}
        avail = {e: [] for e in engs}
        free = {e: 0.0 for e in engs}
        for i in range(n):
            if npred[i] == 0:
                heapq.heappush(fut[pend[i][1]], (0.0, i))
        order = []
        done = 0
        cur_tab = [None]
        while done < n:
            best = None
            for e in engs:
                t = free[e]
                while fut[e] and fut[e][0][0] <= t:
                    rt, i = heapq.heappop(fut[e])
                    heapq.heappush(avail[e], (-prio[i], i))
                if avail[e]:
                    cand = (t, e)
                elif fut[e]:
                    cand = (fut[e][0][0], e)
                else:
                    continue
                if best is None or cand < best:
                    best = cand
            t, e = best
            if not avail[e]:
                rt, i = heapq.heappop(fut[e])
                heapq.heappush(avail[e], (-prio[i], i))
                while fut[e] and fut[e][0][0] <= t:
                    rt2, i2 = heapq.heappop(fut[e])
                    heapq.heappush(avail[e], (-prio[i2], i2))
            _, i = heapq.heappop(avail[e])
            tsw = 0.0
            if e == "act":
                if pend[i][6] is not None and pend[i][6] != cur_tab[0]:
                    held = [(-prio[i], i)]
                    found = None
                    while avail[e] and len(held) < 12:
                        pj, j = heapq.heappop(avail[e])
                        if pend[j][6] is None or pend[j][6] == cur_tab[0]:
                            found = j
                            break
                        held.append((pj, j))
                    if found is not None:
                        i = found
                        for it in held:
                            heapq.heappush(avail[e], it)
                    else:
                        for it in held[1:]:
                            heapq.heappush(avail[e], it)
                if pend[i][6] is not None and pend[i][6] != cur_tab[0]:
                    cur_tab[0] = pend[i][6]
                    tsw = 1.3
            start = max(free[e], ready_t[i]) + tsw
            finish[i] = start + pend[i][5]
            free[e] = finish[i]
            order.append(i)
            done += 1
            for s in succs[i]:
                lat = 0.08 if pend[s][1] == e else 0.2
                if finish[i] + lat > ready_t[s]:
                    ready_t[s] = finish[i] + lat
                npred[s] -= 1
                if npred[s] == 0:
                    heapq.heappush(fut[pend[s][1]], (ready_t[s], s))
        self.last_makespan = max(finish) if n else 0.0
        return order

    def finish(self, bufs):
        toks = []
        for b in bufs:
            toks.append(b.last_w)
            toks.extend(b.readers)
        self._wait("sp", toks)

    def emit(self):
        with self.nc.Block() as block:
            def make(name):
                def run(e):
                    for it in self.prog[name]:
                        if it[0] == "w":
                            e.wait_ge(it[1], it[2])
                        else:
                            ins = it[1](e)
                            if it[2] is not None:
                                ins.then_inc(it[2], it[3])
                return run
            block.tensor(make("pe"))
            block.scalar(make("act"))
            block.vector(make("dve"))
            block.gpsimd(make("pool"))
            block.sync(make("sp"))

    def barrier(self):
        toks = [(e, self.cnt[e]) for e in self.ENG if self.cnt[e] > 0]
        toks += [(b, b.dcnt) for b in self.dma_bufs]
        for e in self.ENG:
            self._wait(e, toks)


C_IDENT = 0
C_ONES = 128
C_UINCL = 256
C_MASK01 = 384
C_SU = 512
C_CH = 640
NCONST = 642


def make_consts():
    c = np.zeros((128, NCONST), np.float32)
    c[:, C_IDENT:C_IDENT + 128] = np.eye(128, dtype=np.float32)
    c[:, C_ONES:C_ONES + 128] = 1.0
    i = np.arange(128)
    c[:, C_UINCL:C_UINCL + 128] = (i[:, None] >= i[None, :])
    c[:, C_MASK01:C_MASK01 + 128] = (i[:, None] < i[None, :])
    c[:, C_SU:C_SU + 128] = (i[:, None] > i[None, :]) & ((i[:, None] // 64) == (i[None, :] // 64))
    c[:, C_CH:C_CH + 2] = ((i[:, None] // 64) == np.arange(2)[None, :])
    return c


NW = 3


class Prog:
    def __init__(self, nc, st, n_layers, dbg=None, flags=None):
        self.nc, self.st, self.L = nc, st, n_layers
        self.dbg = dbg or {}
        self.flags = flags or {}
        self.k = K(nc, st)
        self.k.dma_bufs = []
        self._declare_io()
        self._alloc()

    def sb(self, name, shape, dt, st=None):
        self._uid = getattr(self, "_uid", 0) + 1
        t = (st or self.st).enter_context(self.nc.sbuf_tensor(f"{name}_u{self._uid}", shape, dt))
        return t

    def buf(self, name):
        return Buf(name)

    def dma(self, on, out, in_, reads=(), writes=(), sembuf=None):
        sb = sembuf or (writes[0] if writes else reads[0])
        if sb not in self.k.dma_bufs:
            self.k.dma_bufs.append(sb)
        return self.k.dma(on, out, in_, reads=reads, writes=writes, sembuf=sembuf)

    def _declare_io(self):
        nc, L = self.nc, self.L
        def din(name, shape, dt=F32):
            return nc.dram_tensor(name, list(shape), dt, kind="ExternalInput").ap()
        self.x = din("x", [SEQ, D_MODEL])
        self.consts = din("consts", [128, NCONST])
        self.norm_mix = din("norm_mix", [L, 128, 8])
        self.norm_ffn = din("norm_ffn", [L, 128, 8])
        self.norm_final = din("norm_final", [128, 8])
        self.w_in = din("w_in", [L, D_MODEL, IN_TOTAL])
        self.w_branch = din("w_branch", [L, 4, 512, D_MODEL])
        self.cpar = din("cpar", [L, 128, 4, 8])
        self.c_w_a = din("c_w_a", [L, 8, 64, 64])
        self.c_w_x = din("c_w_x", [L, 8, 64, 64])
        self.b_biasT = din("b_biasT", [L, 128, 8, 5, 128])
        self.maskb = din("maskb", [128, 5, 128])
        self.a_w_gk = din("a_w_gk", [L, 16, 256])
        self.a_bgk = din("a_bgk", [L, 128, 256])
        self.a_ng = din("a_ng", [L, 128, 128])
        self.w_out = din("w_out", [L, D_MODEL, D_MODEL])
        self.w_fg = din("w_ffn_gate", [L, D_MODEL, FFN_HIDDEN])
        self.w_fu = din("w_ffn_up", [L, D_MODEL, FFN_HIDDEN])
        self.w_fd = din("w_ffn_down", [L, FFN_HIDDEN, D_MODEL])
        self.out = nc.dram_tensor("out", [SEQ, D_MODEL], F32, kind="ExternalOutput").ap()
        self.out_bufs = []
        self.dbg_out = {}
        self.dbg_bufs = []
        for name, shape in self.dbg.items():
            self.dbg_out[name] = nc.dram_tensor("dbg_" + name, list(shape), F32, kind="ExternalOutput").ap()

    def _alloc(self):
        nc = self.nc
        self.hT = self.sb("hT", [128, 8, SEQ], F32)
        self.b_hT = [[Buf(f"hT{c}_{t}") for t in range(4)] for c in range(8)]
        self.xnT = self.sb("xnT", [128, 8, SEQ], BF16)
        self.b_xn = [Buf(f"xn{t}") for t in range(4)]
        self.yT = self.sb("yT", [128, 8, SEQ], BF16)
        self.b_yT = [[Buf(f"yT{c}_{t}") for t in range(4)] for c in range(8)]
        self.cst = self.sb("cst", [128, NCONST], F32)
        self.b_cst = Buf("cst")
        self.cbf = self.sb("cbf", [128, NCONST], BF16)
        self.b_cbf = Buf("cbf")
        self.nrm = self.sb("nrm", [128, 3, 8], F32)
        self.b_nrm = Buf("nrm")
        self.wb = [self.sb(f"wb{i}", [128, 4096], BF16) for i in range(NW)]
        self.b_wb = [Buf(f"wb{i}") for i in range(NW)]
        self.wb_i = 0
        self.ps = [self.st.enter_context(nc.psum_tensor(f"ps{i}", [128, 512], F32)) for i in range(8)]
        self.b_ps = [Buf(f"ps{i}") for i in range(8)]
        self.ps_i = 0
        self.k.op("pool", lambda e: e.memset(self.nrm[:], 0.0), writes=[self.b_nrm])

    def psum(self, allowed=range(8)):
        allowed = list(allowed)
        while self.ps_i % 8 not in allowed:
            self.ps_i += 1
        i = self.ps_i % 8
        self.ps_i += 1
        return self.ps[i], self.b_ps[i]

    def run_steps(self, steps, depth=2):
        wsteps = [i for i, s in enumerate(steps) if s[0]]
        assigned = {}
        inflight = {}
        state = {"ptr": 0}

        pinmap = {}

        def try_issue():
            while state["ptr"] < len(wsteps) and len(inflight) < depth:
                si = wsteps[state["ptr"]]
                if len(steps[si]) > 2 and steps[si][2] is not None:
                    reg, li = steps[si][2]
                    if reg not in pinmap:
                        pinmap[reg] = self.wb_i
                        self.wb_i += 2
                    wi = (pinmap[reg] + li) % NW
                else:
                    wi = self.wb_i % NW
                if wi in inflight:
                    break
                if not (len(steps[si]) > 2 and steps[si][2] is not None):
                    self.wb_i += 1
                for dst_fn, src in steps[si][0]:
                    self.dma("pool", dst_fn(self.wb[wi]), src, writes=[self.b_wb[wi]])
                assigned[si] = wi
                inflight[wi] = si
                state["ptr"] += 1

        try_issue()
        for i, s in enumerate(steps):
            if s[0]:
                if i not in assigned:
                    try_issue()
                assert i in assigned, "weight tile not issued"
                wi = assigned[i]
                s[1](self.wb[wi], self.b_wb[wi])
                del inflight[wi]
                try_issue()
            else:
                s[1](None, None)

    def arena_open(self):
        self.ar = ExitStack()
        self.ar_bufs = []

    def arena_close(self, hard=False):
        if not hasattr(self.k, "free_dsems"):
            self.k.free_dsems = {"sw": [], "hw": []}
        if hard or not self.flags.get("soft_arena", True):
            self.k.barrier()
            self.gen_tokens = []
        else:
            best = {}
            for b in self.ar_bufs:
                for tok in [b.last_w] + list(b.readers):
                    if tok is None:
                        continue
                    key, val = tok
                    if best.get(key, 0) < val:
                        best[key] = val
            self.gen_tokens = list(best.items())
        for b in self.ar_bufs:
            if b.dsem is not None:
                self.k.free_dsems[b.name[:2]].append((b.dsem, b.dcnt))
                if b in self.k.dma_bufs:
                    self.k.dma_bufs.remove(b)
        self.ar.close()
        self.ar = None

    def abuf(self, name):
        b = Buf(name)
        b.readers = list(getattr(self, "gen_tokens", []))
        self.ar_bufs.append(b)
        return b

    def at(self, name, shape, dt):
        b = self.abuf(name)
        return self.sb(name, shape, dt, st=self.ar), b

    def ph_load(self):
        k = self.k
        self.dma("sp", self.cst[:], self.consts[:, :], writes=[self.b_cst])
        self.dma("pool", self.cbf[:], self.consts[:, :], writes=[self.b_cbf])
        self.dma("sp", self.nrm[:, 2, :], self.norm_final[:, :], writes=[self.b_nrm])
        self.arena_open()
        stg = [self.at(f"xstg{i}", [128, D_MODEL], F32) for i in range(4)]
        ident = self.cst[:, C_IDENT:C_IDENT + 128]
        for tb in range(16):
            t, b = stg[tb % 4]
            self.dma("sp", t[:], self.x[tb * 128:(tb + 1) * 128, :], writes=[b])
            for half in range(2):
                ps, bps = self.psum()
                fns = []
                for j in range(4):
                    c = half * 4 + j
                    fns.append(lambda e, ps=ps, t=t, c=c, j=j: e.transpose(
                        out=ps[:, j * 128:(j + 1) * 128], in_=t[:, c * 128:(c + 1) * 128], identity=ident))
                k.group("pe", fns, reads=[b, self.b_cst], writes=[bps])
                tt = tb // 4
                dst = self.hT[:, half * 4:half * 4 + 4, tb * 128:(tb + 1) * 128]
                wr = [self.b_hT[half * 4 + j][tt] for j in range(4)]
                eng = "dve" if half == 0 else "act"
                if eng == "dve":
                    k.op("dve", lambda e, ps=ps, dst=dst: e.tensor_copy(
                        out=dst, in_=ps[:].rearrange("p (a b) -> p a b", a=4)), reads=[bps], writes=wr)
                else:
                    k.op("act", lambda e, ps=ps, dst=dst: e.activation(
                        out=dst, in_=ps[:].rearrange("p (a b) -> p a b", a=4), func=AF.Copy), reads=[bps], writes=wr)
        self.arena_close()

    def ph_norm(self, gsel, l):
        k = self.k
        if gsel < 2:
            src = (self.norm_mix if gsel == 0 else self.norm_ffn)[l]
            self.dma("sp", self.nrm[:, gsel, :], src, writes=[self.b_nrm])
        self.arena_open()
        sq = [self.at(f"sq{i}", [128, 512], BF16) for i in range(3)]
        lnb = self.at("lnb", [128, 512], F32)
        rstd = [self.at(f"rstd{i}", [128, 512], F32) for i in range(2)]
        ones = self.cbf[:, C_ONES:C_ONES + 128]
        for tt in range(4):
            ts = slice(tt * 512, (tt + 1) * 512)
            ps, bps = self.psum()
            for c in range(8):
                s, bs = sq[c % 3]
                k.op("act", lambda e, s=s, c=c, ts=ts: e.activation(out=s[:], in_=self.hT[:, c, ts], func=AF.Square),
                     reads=[self.b_hT[c][tt]], writes=[bs])
                k.op("pe", lambda e, ps=ps, s=s, c=c: e.matmul(ps[:], lhsT=ones, rhs=s[:], start=(c == 0), stop=(c == 7)),
                     reads=[bs, self.b_cbf], writes=[bps])
            r, br = rstd[tt % 2]
            k.op("act", lambda e, ps=ps: e.activation(out=lnb[0][:], in_=ps[:], func=AF.Ln, scale=1.0 / D_MODEL, bias=EPS),
                 reads=[bps], writes=[lnb[1]])
            k.op("act", lambda e, r=r: e.activation(out=r[:], in_=lnb[0][:], func=AF.Exp, scale=-0.5),
                 reads=[lnb[1]], writes=[br])
            for c in range(8):
                k.op("dve", lambda e, c=c, ts=ts, r=r: e.scalar_tensor_tensor(
                    out=self.xnT[:, c, ts], in0=self.hT[:, c, ts], scalar=self.nrm[:, gsel, c:c + 1], in1=r[:],
                    op0=ALU.mult, op1=ALU.mult), reads=[self.b_hT[c][tt], br, self.b_nrm], writes=[self.b_xn[tt]])
        self.arena_close()

    def ph_ffn(self, l, steps):
        k = self.k
        NJ = FFN_HIDDEN // 128
        groups = [(0, 8), (8, 16), (16, 22)]
        st = {}

        def begin(w, bw):
            self.arena_open()
            st["act"] = self.sb("ffn_act", [128, 8, SEQ], BF16, st=self.ar)
            st["b_act"] = [[self.abuf(f"act{j}_{t}") for t in range(4)] for j in range(8)]
            st["sg"] = [self.at(f"ffn_sg{i}", [128, 512], F32) for i in range(3)]
            st["sgi"] = 0
        steps.append(([], begin))

        def gu_step(j0, jl0):
            def fn(w, bw):
                wv = w[:].rearrange("p (a b) -> p a b", a=8)
                for jj in range(2):
                    jl = jl0 + jj
                    for tt in range(4):
                        ts = slice(tt * 512, (tt + 1) * 512)
                        pg, bpg = self.psum()
                        pu, bpu = self.psum()
                        k.group("pe", [lambda e, kc=kc, pg=pg, jj=jj, ts=ts: e.matmul(
                            pg[:], lhsT=wv[:, kc, jj * 128:(jj + 1) * 128], rhs=self.xnT[:, kc, ts],
                            start=(kc == 0), stop=(kc == 7)) for kc in range(8)],
                            reads=[bw, self.b_xn[tt]], writes=[bpg])
                        k.group("pe", [lambda e, kc=kc, pu=pu, jj=jj, ts=ts: e.matmul(
                            pu[:], lhsT=wv[:, kc, 256 + jj * 128:256 + (jj + 1) * 128], rhs=self.xnT[:, kc, ts],
                            start=(kc == 0), stop=(kc == 7)) for kc in range(8)],
                            reads=[bw, self.b_xn[tt]], writes=[bpu])
                        sg, bsg = st["sg"][st["sgi"] % 3]
                        st["sgi"] += 1
                        k.op("act", lambda e, sg=sg, pg=pg: e.activation(out=sg[:], in_=pg[:], func=AF.Silu),
                             reads=[bpg], writes=[bsg])
                        k.op("dve", lambda e, sg=sg, pu=pu, jl=jl, ts=ts: e.tensor_tensor(
                            out=st["act"][:, jl, ts], in0=sg[:], in1=pu[:], op=ALU.mult),
                            reads=[bsg, bpu], writes=[st["b_act"][jl][tt]])
            loads = [
                (lambda w: w[:].rearrange("p (a b) -> p a b", a=8)[:, :, 0:256],
                 self.w_fg[l][:, j0 * 128:j0 * 128 + 256].rearrange("(kc kp) n -> kp kc n", kp=128)),
                (lambda w: w[:].rearrange("p (a b) -> p a b", a=8)[:, :, 256:512],
                 self.w_fu[l][:, j0 * 128:j0 * 128 + 256].rearrange("(kc kp) n -> kp kc n", kp=128)),
            ]
            return (loads, fn)

        def down_step(ja, jb, half):
            J = jb - ja
            def fn(w, bw):
                wv = w[:].rearrange("p (a b) -> p a b", a=8)
                for c4 in range(4):
                    c2 = half * 4 + c4
                    for tt in range(4):
                        ts = slice(tt * 512, (tt + 1) * 512)
                        ps, bps = self.psum()
                        k.group("pe", [lambda e, jl=jl, ps=ps, c4=c4, ts=ts: e.matmul(
                            ps[:], lhsT=wv[:, jl, c4 * 128:(c4 + 1) * 128], rhs=st["act"][:, jl, ts],
                            start=(jl == 0), stop=(jl == J - 1)) for jl in range(J)],
                            reads=[bw] + [st["b_act"][jl][tt] for jl in range(J)], writes=[bps])
                        k.op("dve", lambda e, ps=ps, c2=c2, ts=ts: e.tensor_tensor(
                            out=self.hT[:, c2, ts], in0=self.hT[:, c2, ts], in1=ps[:], op=ALU.add),
                            reads=[bps], writes=[self.b_hT[c2][tt]])
            loads = [(lambda w: w[:].rearrange("p (a b) -> p a b", a=8)[:, 0:J, :],
                      self.w_fd[l][ja * 128:jb * 128, half * 512:(half + 1) * 512].rearrange("(j kp) n -> kp j n", kp=128))]
            return (loads, fn)

        for (ja, jb) in groups:
            for j0 in range(ja, jb, 2):
                steps.append(gu_step(j0, j0 - ja))
            for half in range(2):
                steps.append(down_step(ja, jb, half))

        def end(w, bw):
            self.arena_close()
        steps.append(([], end))

    def ph_final(self):
        k = self.k
        self.arena_open()
        sq = [self.at(f"fsq{i}", [128, 512], BF16) for i in range(3)]
        lnb = self.at("flnb", [128, 512], F32)
        rstd = self.at("frstd", [128, 512], F32)
        yt = [self.at(f"fyt{i}", [128, 512], F32) for i in range(2)]
        stg = [self.at(f"fstg{i}", [128, 4, D_MODEL], F32) for i in range(2)]
        ones = self.cbf[:, C_ONES:C_ONES + 128]
        ident = self.cst[:, C_IDENT:C_IDENT + 128]
        for tt in range(4):
            ts = slice(tt * 512, (tt + 1) * 512)
            ps, bps = self.psum()
            for c in range(8):
                s, bs = sq[c % 3]
                k.op("act", lambda e, s=s, c=c, ts=ts: e.activation(out=s[:], in_=self.hT[:, c, ts], func=AF.Square),
                     reads=[self.b_hT[c][tt]], writes=[bs])
                k.op("pe", lambda e, ps=ps, s=s, c=c: e.matmul(ps[:], lhsT=ones, rhs=s[:], start=(c == 0), stop=(c == 7)),
                     reads=[bs, self.b_cbf], writes=[bps])
            r, br = rstd
            k.op("act", lambda e, ps=ps: e.activation(out=lnb[0][:], in_=ps[:], func=AF.Ln, scale=1.0 / D_MODEL, bias=EPS),
                 reads=[bps], writes=[lnb[1]])
            k.op("act", lambda e, r=r: e.activation(out=r[:], in_=lnb[0][:], func=AF.Exp, scale=-0.5),
                 reads=[lnb[1]], writes=[br])
            sg, bsg = stg[tt % 2]
            for c in range(8):
                y, by = yt[c % 2]
                k.op("dve", lambda e, c=c, ts=ts, r=r, y=y: e.scalar_tensor_tensor(
                    out=y[:], in0=self.hT[:, c, ts], scalar=self.nrm[:, 2, c:c + 1], in1=r[:],
                    op0=ALU.mult, op1=ALU.mult), reads=[self.b_hT[c][tt], br, self.b_nrm], writes=[by])
                pt, bpt = self.psum()
                k.group("pe", [lambda e, pt=pt, y=y, j=j: e.transpose(
                    out=pt[:, j * 128:(j + 1) * 128], in_=y[:, j * 128:(j + 1) * 128], identity=ident) for j in range(4)],
                    reads=[by, self.b_cst], writes=[bpt])
                k.op("act", lambda e, pt=pt, sg=sg, c=c: e.activation(
                    out=sg[:, :, c * 128:(c + 1) * 128], in_=pt[:].rearrange("p (a b) -> p a b", a=4), func=AF.Copy),
                    reads=[bpt], writes=[bsg])
            bo = Buf(f"out{tt}")
            self.out_bufs.append(bo)
            self.dma("sp", self.out[tt * 512:(tt + 1) * 512, :].rearrange("(tb p) d -> p tb d", p=128), sg[:],
                     reads=[bsg], writes=[bo], sembuf=bsg)
        self.arena_close()


    def build(self):
        self.ph_load()
        steps = []
        for l in range(self.L):
            if self.flags.get("mixer", True):
                steps.append(([], lambda w, bw, l=l: self.ph_norm(0, l)))
                self.ph_mixers(l, steps)
            if self.flags.get("ffn", True):
                steps.append(([], lambda w, bw, l=l: self.ph_norm(1, l)))
                self.ph_ffn(l, steps)
        self.run_steps(steps)
        self.ph_final()
        self.k.finish(self.out_bufs + list(self.dbg_bufs))
        self.k.emit()


def build_program(n_layers=DEPTH, dbg=None, flags=None):
    nc = bass.Bass("TRN2", target_bir_lowering=False)
    with ExitStack() as st:
        p = Prog(nc, st, n_layers, dbg=dbg, flags=flags)
        p.build()
        stats = (p.k.n_inst, p.k.n_wait)
    return nc, stats


def fm_vec(v):
    return np.ascontiguousarray(np.asarray(v, np.float32).reshape(8, 128).T)


def make_cpar(inputs, L):
    out = np.zeros((L, 128, 4, 8), np.float32)
    for l in range(L):
        cols = [inputs["c_conv_w"][l][j] for j in range(4)] + [inputs["c_conv_b"][l], inputs["c_b_a"][l],
                                                               inputs["c_b_x"][l], inputs["c_lambda"][l]]
        for i, v in enumerate(cols):
            out[l, :, :, i] = np.asarray(v, np.float32).reshape(4, 128).T
    return out


def make_biasT(inputs, L):
    kk = np.arange(128)[:, None, None]
    jj = np.arange(5)[None, :, None]
    qq = np.arange(128)[None, None, :]
    idx = np.clip(qq - kk + (4 - jj) * 128, -128, 128) + 128
    tab = np.asarray(inputs["b_rel_bias"], np.float32)[:L]
    g = tab[:, :, idx]
    return np.ascontiguousarray(g.transpose(0, 2, 1, 3, 4))


def make_maskb():
    kk = np.arange(128)[:, None, None]
    jj = np.arange(5)[None, :, None]
    qq = np.arange(128)[None, None, :]
    diff = 8 - 2 * jj + (qq >= 64) - (kk >= 64)
    return np.where((diff >= 0) & (diff <= 8), 0.0, -30000.0).astype(np.float32)


def make_in_maps(inputs, n_layers=DEPTH):
    L = n_layers
    f = lambda a: np.ascontiguousarray(np.asarray(a, np.float32))
    shared = {
        "consts": make_consts(),
        "norm_mix": np.stack([fm_vec(inputs["norm_mix"][l]) for l in range(L)]),
        "norm_ffn": np.stack([fm_vec(inputs["norm_ffn"][l]) for l in range(L)]),
        "norm_final": fm_vec(inputs["norm_final"]),
        "w_in": f(inputs["w_in"][:L]),
        "w_branch": f(inputs["w_branch"][:L]),
        "cpar": make_cpar(inputs, L),
        "c_w_a": f(inputs["c_w_a"][:L]),
        "c_w_x": f(inputs["c_w_x"][:L]),
        "b_biasT": make_biasT(inputs, L),
        "maskb": make_maskb(),
        "a_w_gk": f(inputs["a_w_gk"][:L]),
        "a_bgk": np.ascontiguousarray(np.broadcast_to(np.asarray(inputs["a_b_gk"], np.float32)[:L, None, :], (L, 128, 256))),
        "a_ng": np.ascontiguousarray(np.broadcast_to(np.asarray(inputs["a_norm"], np.float32)[:L, None, :], (L, 128, 128))),
        "w_out": f(inputs["w_out"][:L]),
        "w_ffn_gate": f(inputs["w_ffn_gate"][:L]),
        "w_ffn_up": f(inputs["w_ffn_up"][:L]),
        "w_ffn_down": f(inputs["w_ffn_down"][:L]),
    }
    x = f(inputs["x"])
    return [dict(shared, x=x[b]) for b in range(x.shape[0])]


_CACHE = {}


def kernel(**inputs):
    if "nc" not in _CACHE:
        _CACHE["nc"] = build_program(DEPTH)[0]
    nc = _CACHE["nc"]
    in_maps = make_in_maps(inputs)
    res = run_bass_kernel_spmd(nc, in_maps, core_ids=list(range(NCORES)))
    return np.stack([np.asarray(r["out"], np.float32) for r in res.results], axis=0)


def _ph_mixers(self, l, steps):
    en = self.flags.get("branches", (1, 1, 1, 1))
    fns = [self.mix_a, self.mix_b, self.mix_c, self.mix_d]

    def zero_step(slot):
        def zero(w, bw, slot=slot):
            for c in range(4):
                self.k.op("pool", lambda e, c=c: e.memset(self.yT[:, slot * 4 + c, :], 0.0),
                          writes=[self.b_yT[slot * 4 + c][t] for t in range(4)])
        return ([], zero)

    for slot, n in enumerate((0, 3)):
        if en[n]:
            fns[n](l, steps, slot)
        else:
            steps.append(zero_step(slot))
    self.ph_merge(l, steps, (0, 3))
    if en[1] and en[2] and self.flags.get("cosched", True):
        sb, sc = [], []
        self.mix_b(l, sb, 0, ext=True)
        self.mix_c(l, sc, 1, ext=True)
        steps.append(([], lambda w, bw: self.arena_open()))
        steps.append(sb[0])
        steps.append(sc[0])
        B_, C_ = ((l, 0),), ((l, 1),)
        order = [sb[1] + B_, sc[1] + C_, sb[2] + B_, sb[3] + B_, sc[2] + C_, sb[4] + B_]

        def first(w, bw, f=order[0][1]):
            self.k.sched_begin()
            f(w, bw)

        def last(w, bw, f=order[-1][1]):
            f(w, bw)
            self.k.sched_end(("bc", 0))
        order[0] = (order[0][0], first, order[0][2])
        order[-1] = (order[-1][0], last, order[-1][2])
        steps.extend(order)
        steps.append(([], lambda w, bw: self.arena_close()))
    else:
        for slot, n in enumerate((1, 2)):
            if en[n]:
                fns[n](l, steps, slot)
            else:
                steps.append(zero_step(slot))
    self.ph_merge(l, steps, (1, 2))


def _ph_merge(self, l, steps, pair):
    k = self.k
    st = {}

    def begin(w, bw):
        self.arena_open()
        st["T1"] = self.sb("mg_T1", [128, 2, SEQ], F32, st=self.ar)
        st["b_T1"] = [[self.abuf(f"T1_{c}_{t}") for t in range(4)] for c in range(2)]
        st["M"] = self.sb("mg_M", [128, 4, SEQ], BF16, st=self.ar)
        st["b_M"] = [[self.abuf(f"M_{c}_{t}") for t in range(4)] for c in range(4)]
        st["s"] = [self.at(f"mg_s{i}", [128, 512], F32) for i in range(3)]
        st["t2"] = [self.at(f"mg_t{i}", [128, 512], F32) for i in range(2)]
        st["i"] = 0
    steps.append(([], begin))

    def gw_step(slot, cp):
        n = pair[slot]
        c0 = cp * 2

        def fn(w, bw):
            gv = w[:, 0:2048].rearrange("p (a b) -> p a b", a=8)
            bv = w[:, 2048:3072].rearrange("p (a b) -> p a b", a=4)
            for cc in range(2):
                c = c0 + cc
                for tt in range(4):
                    ts = slice(tt * 512, (tt + 1) * 512)
                    pg, bpg = self.psum()
                    pw, bpw = self.psum()
                    k.group("pe", [lambda e, kc=kc, pg=pg, cc=cc, ts=ts: e.matmul(
                        pg[:], lhsT=gv[:, kc, cc * 128:(cc + 1) * 128], rhs=self.xnT[:, kc, ts],
                        start=(kc == 0), stop=(kc == 7)) for kc in range(8)],
                        reads=[bw, self.b_xn[tt]], writes=[bpg])
                    k.group("pe", [lambda e, kc=kc, pw=pw, cc=cc, ts=ts: e.matmul(
                        pw[:], lhsT=bv[:, kc, cc * 128:(cc + 1) * 128], rhs=self.yT[:, slot * 4 + kc, ts],
                        start=(kc == 0), stop=(kc == 3)) for kc in range(4)],
                        reads=[bw] + [self.b_yT[slot * 4 + kc][tt] for kc in range(4)], writes=[bpw])
                    s, bs = st["s"][st["i"] % 3]
                    st["i"] += 1
                    k.op("act", lambda e, s=s, pg=pg: e.activation(out=s[:], in_=pg[:], func=AF.Sigmoid),
                         reads=[bpg], writes=[bs])
                    if slot == 0:
                        k.op("dve", lambda e, s=s, pw=pw, cc=cc, ts=ts: e.tensor_tensor(
                            out=st["T1"][:, cc, ts], in0=s[:], in1=pw[:], op=ALU.mult),
                            reads=[bs, bpw], writes=[st["b_T1"][cc][tt]])
                    else:
                        t2, bt2 = st["t2"][st["i"] % 2]
                        k.op("dve", lambda e, s=s, pw=pw, t2=t2: e.tensor_tensor(
                            out=t2[:], in0=s[:], in1=pw[:], op=ALU.mult), reads=[bs, bpw], writes=[bt2])
                        cm = c % 4
                        k.op("pool", lambda e, t2=t2, cc=cc, cm=cm, ts=ts: e.tensor_tensor(
                            out=st["M"][:, cm, ts], in0=st["T1"][:, cc, ts], in1=t2[:], op=ALU.add),
                            reads=[bt2, st["b_T1"][cc][tt]], writes=[st["b_M"][cm][tt]])
        loads = [
            (lambda w: w[:, 0:2048].rearrange("p (a b) -> p a b", a=8),
             self.w_in[l][:, O_GATE + n * 1024 + c0 * 128:O_GATE + n * 1024 + c0 * 128 + 256].rearrange(
                 "(kc kp) n -> kp kc n", kp=128)),
            (lambda w: w[:, 2048:3072].rearrange("p (a b) -> p a b", a=4),
             self.w_branch[l][n][:, c0 * 128:c0 * 128 + 256].rearrange("(kc kp) n -> kp kc n", kp=128)),
        ]
        return (loads, fn)

    def out_step(cg):
        def fn(w, bw):
            wv = w[:].rearrange("p (a b) -> p a b", a=4)
            for c2 in range(8):
                for tt in range(4):
                    ts = slice(tt * 512, (tt + 1) * 512)
                    ps, bps = self.psum()
                    k.group("pe", [lambda e, j=j, ps=ps, c2=c2, ts=ts: e.matmul(
                        ps[:], lhsT=wv[:, j, c2 * 128:(c2 + 1) * 128], rhs=st["M"][:, j, ts],
                        start=(j == 0), stop=(j == 3)) for j in range(4)],
                        reads=[bw] + [st["b_M"][j][tt] for j in range(4)], writes=[bps])
                    k.op("dve", lambda e, ps=ps, c2=c2, ts=ts: e.tensor_tensor(
                        out=self.hT[:, c2, ts], in0=self.hT[:, c2, ts], in1=ps[:], op=ALU.add),
                        reads=[bps], writes=[self.b_hT[c2][tt]])
        loads = [(lambda w: w[:].rearrange("p (a b) -> p a b", a=4),
                  self.w_out[l][cg * 512:(cg + 1) * 512, :].rearrange("(kc kp) n -> kp kc n", kp=128))]
        return (loads, fn)

    for cg in range(2):
        for cpl in range(2):
            cp = cg * 2 + cpl
            steps.append(gw_step(0, cp))
            steps.append(gw_step(1, cp))
        steps.append(out_step(cg))

    def end(w, bw):
        self.arena_close()
    steps.append(([], end))


Prog.ph_mixers = _ph_mixers
Prog.ph_merge = _ph_merge


def _mix_c(self, l, steps, slot, ext=False):
    k = self.k
    st = {}
    PSC = range(5, 8) if ext else range(8)
    N = 512

    NSET = 1 if ext else 2
    NXIN = 1 if ext else 2

    def begin(w, bw):
        if not ext:
            self.arena_open()
        st["cp"] = self.at("c_par", [128, 4, 8], F32)
        st["cl"] = self.at("c_cl", [128, 4, 2], F32)
        st["wbd"] = self.at("c_wbd", [128, 8, 128], BF16)
        st["xin"] = [self.at(f"c_xin{i}", [128, 3 + SEQ], F32) for i in range(NXIN)]
        names = ("gg", "xc", "ra", "ibx", "a2", "hs")
        st["tmp"] = [{nm: self.at(f"c_{nm}{i}", [128, N], F32) for nm in names} for i in range(NSET)]
        for i in range(NSET):
            st["tmp"][i]["xcb"] = self.at(f"c_xcb{i}", [128, N], BF16)
        st["ui"] = 0
        cp, bcp = st["cp"]
        cl, bcl = st["cl"]
        wbd, bwbd = st["wbd"]
        self.dma("sp", cp[:], self.cpar[l], writes=[bcp])
        hs_t, bwst = st["tmp"][0]["hs"]
        wst = hs_t[:, 0:512].rearrange("p (a b) -> p a b", a=8)
        k.op("dve", lambda e: e.memset(wbd[:], 0.0), writes=[bwbd])
        for which, srcw in enumerate((self.c_w_a, self.c_w_x)):
            for g in range(2):
                self.dma("sp", wst[g * 64:(g + 1) * 64, which * 4:(which + 1) * 4, :],
                         srcw[l].rearrange("(cc g) r c -> g r cc c", g=2)[g], writes=[bwst])
        for g in range(2):
            k.op("dve", lambda e, g=g: e.tensor_copy(out=wbd[g * 64:(g + 1) * 64, :, g * 64:(g + 1) * 64],
                                                     in_=wst[g * 64:(g + 1) * 64, :, :]),
                 reads=[bwst], writes=[bwbd])
        k.op("act", lambda e: e.activation(out=cl[:, :, 0], in_=cp[:, :, 7], func=AF.Exp, scale=-1.0),
             reads=[bcp], writes=[bcl])
        k.op("act", lambda e: e.activation(out=cl[:, :, 0], in_=cl[:, :, 0], func=AF.Ln, bias=1.0),
             reads=[bcl], writes=[bcl])
        k.op("dve", lambda e: e.tensor_scalar(out=cl[:, :, 1], in0=cl[:, :, 0], scalar1=-16.0, scalar2=None, op0=ALU.mult),
             reads=[bcl], writes=[bcl])
        k.op("dve", lambda e: e.tensor_scalar(out=cl[:, :, 0], in0=cl[:, :, 0], scalar1=-8.0, scalar2=None, op0=ALU.mult),
             reads=[bcl], writes=[bcl])
        for i in range(NXIN):
            xin, bxin = st["xin"][i]
            k.op("dve", lambda e, xin=xin: e.memset(xin[:, 0:3], 0.0), writes=[bxin])
    steps.append(([], begin))

    def chunk_step(cpair):
        def fn(w, bw):
            if ext:
                return fn_(w, bw)
            k.sched_begin()
            fn_(w, bw)
            k.sched_end(("c", cpair))

        def fn_(w, bw):
            wv = w[:].rearrange("p (a b) -> p a b", a=8)
            cp, bcp = st["cp"]
            cl, bcl = st["cl"]
            wbd, bwbd = st["wbd"]
            for j in range(2):
                cc = cpair * 2 + j
                xin, bxin = st["xin"][cc % NXIN]
                for tt in range(4):
                    ts = slice(tt * 512, (tt + 1) * 512)
                    px, bpx = self.psum(PSC)
                    k.group("pe", [lambda e, kc=kc, px=px, j=j, ts=ts: e.matmul(
                        px[:], lhsT=wv[:, kc, 256 + j * 128:256 + (j + 1) * 128], rhs=self.xnT[:, kc, ts],
                        start=(kc == 0), stop=(kc == 7)) for kc in range(8)],
                        reads=[bw, self.b_xn[tt]], writes=[bpx])
                    k.op("dve", lambda e, px=px, xin=xin, tt=tt: e.tensor_copy(
                        out=xin[:, 3 + tt * 512:3 + (tt + 1) * 512], in_=px[:]),
                        reads=[bpx], writes=[bxin])
                prev_hs = None
                for tt in range(4):
                    ts = slice(tt * 512, (tt + 1) * 512)
                    T = st["tmp"][st["ui"] % NSET]
                    st["ui"] += 1
                    gg, bgg = T["gg"]; xc, bxc = T["xc"]; xcb, bxcb = T["xcb"]; ra, bra = T["ra"]
                    ibx, bibx = T["ibx"]; a2, ba2 = T["a2"]; hs, bhs = T["hs"]
                    pgt, bpgt = self.psum(PSC)
                    k.group("pe", [lambda e, kc=kc, pgt=pgt, j=j, ts=ts: e.matmul(
                        pgt[:], lhsT=wv[:, kc, j * 128:(j + 1) * 128], rhs=self.xnT[:, kc, ts],
                        start=(kc == 0), stop=(kc == 7)) for kc in range(8)],
                        reads=[bw, self.b_xn[tt]], writes=[bpgt])
                    k.op("act", lambda e, gg=gg, pgt=pgt: e.activation(out=gg[:], in_=pgt[:], func=AF.Gelu_apprx_tanh),
                         reads=[bpgt], writes=[bgg], tab="gelu")
                    o = tt * 512
                    k.op("dve", lambda e, xc=xc, xin=xin, o=o, cc=cc: e.tensor_scalar(
                        out=xc[:], in0=xin[:, o + 3:o + 3 + N], scalar1=cp[:, cc, 3:4], scalar2=cp[:, cc, 4:5],
                        op0=ALU.mult, op1=ALU.add), reads=[bxin, bcp], writes=[bxc])
                    for jj in range(3):
                        k.op("dve", lambda e, xc=xc, xin=xin, o=o, cc=cc, jj=jj: e.scalar_tensor_tensor(
                            out=xc[:], in0=xin[:, o + jj:o + jj + N], scalar=cp[:, cc, jj:jj + 1], in1=xc[:],
                            op0=ALU.mult, op1=ALU.add), reads=[bxin, bcp, bxc], writes=[bxc])
                    k.op("dve", lambda e, xc=xc, xcb=xcb: e.tensor_copy(out=xcb[:], in_=xc[:]), reads=[bxc], writes=[bxcb])
                    pr, bpr = self.psum(PSC)
                    k.op("pe", lambda e, pr=pr, xcb=xcb, cc=cc: e.matmul(pr[:], lhsT=wbd[:, cc, :], rhs=xcb[:], start=True, stop=True),
                         reads=[bwbd, bxcb], writes=[bpr])
                    pi, bpi = self.psum(PSC)
                    k.op("pe", lambda e, pi=pi, xcb=xcb, cc=cc: e.matmul(pi[:], lhsT=wbd[:, 4 + cc, :], rhs=xcb[:], start=True, stop=True),
                         reads=[bwbd, bxcb], writes=[bpi])
                    k.op("act", lambda e, ra=ra, pr=pr, cc=cc: e.activation(out=ra[:], in_=pr[:], func=AF.Sigmoid, bias=cp[:, cc, 5:6]),
                         reads=[bpr, bcp], writes=[bra], tab="sig")
                    k.op("act", lambda e, ibx=ibx, pi=pi, cc=cc: e.activation(out=ibx[:], in_=pi[:], func=AF.Sigmoid, bias=cp[:, cc, 6:7]),
                         reads=[bpi, bcp], writes=[bibx], tab="sig")
                    k.op("act", lambda e, ra=ra, cc=cc: e.activation(out=ra[:], in_=ra[:], func=AF.Exp, scale=cl[:, cc, 0:1]),
                         reads=[bra, bcl], writes=[bra], tab="expln")
                    k.op("pool", lambda e, a2=a2, ra=ra: e.tensor_tensor(out=a2[:], in0=ra[:], in1=ra[:], op=ALU.mult),
                         reads=[bra], writes=[ba2])
                    k.op("act", lambda e, a2=a2: e.activation(out=a2[:], in_=a2[:], func=AF.Ln, scale=-1.0, bias=1.0),
                         reads=[ba2], writes=[ba2], tab="expln")
                    k.op("act", lambda e, a2=a2: e.activation(out=a2[:], in_=a2[:], func=AF.Exp, scale=0.5),
                         reads=[ba2], writes=[ba2], tab="expln")
                    k.op("dve", lambda e, ibx=ibx, xc=xc: e.tensor_tensor(out=ibx[:], in0=ibx[:], in1=xc[:], op=ALU.mult),
                         reads=[bibx, bxc], writes=[bibx])
                    k.op("pool", lambda e, ibx=ibx, a2=a2: e.tensor_tensor(out=ibx[:], in0=ibx[:], in1=a2[:], op=ALU.mult),
                         reads=[bibx, ba2], writes=[bibx])
                    init = 0.0 if prev_hs is None else prev_hs[0][:, N - 1:N]
                    rd = [bra, bibx] + ([prev_hs[1]] if prev_hs is not None else [])
                    k.op("dve", lambda e, hs=hs, ra=ra, ibx=ibx, init=init: e.tensor_tensor_scan(
                        out=hs[:], data0=ra[:], data1=ibx[:], initial=init, op0=ALU.mult, op1=ALU.add),
                        reads=rd, writes=[bhs])
                    prev_hs = (hs, bhs)
                    k.op("pool", lambda e, hs=hs, gg=gg, cc=cc, ts=ts: e.tensor_tensor(
                        out=self.yT[:, slot * 4 + cc, ts], in0=hs[:], in1=gg[:], op=ALU.mult),
                        reads=[bhs, bgg], writes=[self.b_yT[slot * 4 + cc][tt]])
        loads = [
            (lambda w: w[:].rearrange("p (a b) -> p a b", a=8)[:, :, 0:256],
             self.w_in[l][:, O_CG + cpair * 256:O_CG + cpair * 256 + 256].rearrange("(kc kp) n -> kp kc n", kp=128)),
            (lambda w: w[:].rearrange("p (a b) -> p a b", a=8)[:, :, 256:512],
             self.w_in[l][:, O_CX + cpair * 256:O_CX + cpair * 256 + 256].rearrange("(kc kp) n -> kp kc n", kp=128)),
        ]
        return (loads, fn)

    for cpair in range(2):
        steps.append(chunk_step(cpair))

    def end(w, bw):
        self.arena_close()
    if not ext:
        steps.append(([], end))


Prog.mix_c = _mix_c
Prog.mix_a = _mix_c
Prog.mix_b = _mix_c
Prog.mix_d = _mix_c


def pipeline(n, stages):
    ns = len(stages)
    for t in range(n + ns - 1):
        for s in range(ns):
            i = t - s
            if 0 <= i < n:
                stages[s](i)


def _mix_b(self, l, steps, slot, ext=False):
    k = self.k
    st = {}
    PSB = range(5) if ext else range(8)
    assert len(PSB) >= 5

    def begin(w, bw):
        if not ext:
            self.arena_open()
        st["qT"] = self.at("b_qT", [128, SEQ], BF16)
        st["kTp"] = [self.at(f"b_kTp{h}", [128, SEQ], BF16) for h in range(2)]
        st["vt"] = self.at("b_vt", [128, 16, 2, 65], BF16)
        st["bias"] = self.at("b_bias", [128, 5, 128], F32)
        st["mask"] = self.at("b_mask", [128, 5, 128], BF16)
        st["biasb"] = self.at("b_biasb", [128, 2, 5, 128], BF16)
        st["pT"] = [self.at(f"b_pT{i}", [128, 5, 128], BF16) for i in range(3)]
        st["ri"] = [self.at(f"b_ri{i}", [128, 2], F32) for i in range(2)]
        st["ytm"] = [self.at(f"b_ytm{i}", [128, 4, 128], BF16) for i in range(1 if ext else 2)]
        self.dma("pool", st["mask"][0][:], self.maskb[:, :, :], writes=[st["mask"][1]])
        vt, bvt = st["vt"]
        k.op("dve", lambda e: e.memset(vt[:, :, :, 64:65], 1.0), writes=[bvt])
        for h in range(2):
            t, tb_ = st["kTp"][h]
            o = (1 - h) * 64
            k.op("dve", lambda e, t=t, o=o: e.memset(t[o:o + 64, :], 0.0), writes=[tb_])
    steps.append(([], begin))

    def pair_step(p):
        def fn(w, bw):
            if ext:
                return fn_(w, bw)
            k.sched_begin()
            fn_(w, bw)
            k.sched_end(("b", p))

        def fn_(w, bw):
            wv = w[:].rearrange("p (a b) -> p a b", a=8)
            qT, bqT = st["qT"]; kTp = st["kTp"]; vt, bvt = st["vt"]
            bias, bbias = st["bias"]; mask, bmask = st["mask"]; biasb, bbiasb = st["biasb"]
            ident = self.cbf[:, C_IDENT:C_IDENT + 128]
            for hh in range(2):
                self.dma("sp", bias[:], self.b_biasT[l][:, 2 * p + hh, :, :], writes=[bbias])
                k.op("pool", lambda e, hh=hh: e.tensor_tensor(out=biasb[:, hh], in0=bias[:], in1=mask[:], op=ALU.add),
                     reads=[bbias, bmask], writes=[bbiasb], n=640)
            for tt in range(4):
                ts = slice(tt * 512, (tt + 1) * 512)
                for which in range(2):
                    ps, bps = self.psum(PSB)
                    k.group("pe", [lambda e, kc=kc, ps=ps, which=which, ts=ts: e.matmul(
                        ps[:], lhsT=wv[:, kc, which * 128:(which + 1) * 128], rhs=self.xnT[:, kc, ts],
                        start=(kc == 0), stop=(kc == 7)) for kc in range(8)],
                        reads=[bw, self.b_xn[tt]], writes=[bps])
                    if which == 0:
                        k.op("act", lambda e, ps=ps, ts=ts: e.activation(out=qT[:, ts], in_=ps[:], func=AF.Identity, scale=0.125),
                             reads=[bps], writes=[bqT])
                    else:
                        for h in range(2):
                            t, tb_ = kTp[h]
                            k.op("dve", lambda e, ps=ps, ts=ts, t=t, h=h: e.tensor_copy(
                                out=t[h * 64:(h + 1) * 64, ts], in_=ps[h * 64:(h + 1) * 64, :]), reads=[bps], writes=[tb_])
                ps, bps = self.psum(PSB)
                fns = []
                for j in range(4):
                    tb = tt * 4 + j
                    for kc in range(8):
                        fns.append(lambda e, kc=kc, ps=ps, j=j, tb=tb: e.matmul(
                            ps[:, j * 128:(j + 1) * 128], lhsT=self.xnT[:, kc, tb * 128:(tb + 1) * 128], rhs=wv[:, kc, 256:384],
                            start=(kc == 0), stop=(kc == 7)))
                k.group("pe", fns, reads=[bw, self.b_xn[tt]], writes=[bps])
                k.op("act", lambda e, ps=ps, tt=tt: e.activation(
                    out=vt[:, tt * 4:(tt + 1) * 4, :, 0:64], in_=ps[:].rearrange("p (a h d) -> p a h d", a=4, h=2),
                    func=AF.Copy), reads=[bps], writes=[bvt])
            po = None
            for qb in range(16):
                qs = slice(qb * 128, (qb + 1) * 128)
                j0 = max(0, 4 - qb)
                if qb % 4 == 0:
                    ytm, bytm = st["ytm"][(qb // 4) % len(st["ytm"])]
                po, bpo = self.psum(PSB)
                for hh in range(2):
                    hs = slice(hh * 64, (hh + 1) * 64)
                    kt, bkt = kTp[hh]
                    pT, bpT = st["pT"][(qb * 2 + hh) % 3]
                    ps1, bps1 = self.psum(PSB)
                    ps2, bps2 = self.psum(PSB)
                    identb = self.cbf[:, C_IDENT:C_IDENT + 128]
                    fns = []
                    for j in range(j0, 4):
                        kb = qb - 4 + j
                        fns.append(lambda e, j=j, kb=kb, ps1=ps1, kt=kt, qs=qs: e.matmul(
                            ps1[:, j * 128:(j + 1) * 128], lhsT=kt[:, kb * 128:(kb + 1) * 128], rhs=qT[:, qs],
                            start=True, stop=False, skip_group_check=True))
                        fns.append(lambda e, j=j, ps1=ps1, hh=hh: e.matmul(
                            ps1[:, j * 128:(j + 1) * 128], lhsT=identb, rhs=biasb[:, hh, j, :],
                            start=False, stop=True, skip_group_check=True))
                    if fns:
                        k.group("pe", fns, reads=[bkt, bqT, bbiasb, self.b_cbf], writes=[bps1], n=128)
                    k.group("pe", [
                        lambda e, ps2=ps2, kt=kt, qs=qs: e.matmul(ps2[:, 0:128], lhsT=kt[:, qs], rhs=qT[:, qs], start=True, stop=False),
                        lambda e, ps2=ps2, hh=hh: e.matmul(ps2[:, 0:128], lhsT=identb, rhs=biasb[:, hh, 4, :], start=False, stop=True)],
                        reads=[bkt, bqT, bbiasb, self.b_cbf], writes=[bps2], n=128)
                    if j0 < 4:
                        k.op("act", lambda e, pT=pT, ps1=ps1, j0=j0: e.activation(
                            out=pT[:, j0:4, :], in_=ps1[:].rearrange("p (a b) -> p a b", a=4)[:, j0:4, :], func=AF.Exp),
                            reads=[bps1], writes=[bpT], n=(4 - j0) * 128)
                    k.op("act", lambda e, pT=pT, ps2=ps2: e.activation(out=pT[:, 4, :], in_=ps2[:, 0:128], func=AF.Exp),
                         reads=[bps2], writes=[bpT], n=128)
                    k.group("pe", [lambda e, j=j, pT=pT, po=po, hh=hh, qb=qb, j0=j0: e.matmul(
                        po[:, hh * 65:(hh + 1) * 65], lhsT=pT[:, j, :], rhs=vt[:, qb - 4 + j, hh, :],
                        start=(j == j0), stop=(j == 4)) for j in range(j0, 5)],
                        reads=[bpT, bvt], writes=[bpo], n=65)
                ri, bri = st["ri"][qb % 2]
                pov = po[:, 0:130].rearrange("p (h d) -> p h d", h=2)
                k.op("dve", lambda e, ri=ri, pov=pov: e.reciprocal(out=ri[:], in_=pov[:, :, 64]), reads=[bpo], writes=[bri])
                for hh in range(2):
                    k.op("dve", lambda e, ri=ri, pov=pov, hh=hh, ytm=ytm, qb=qb: e.tensor_scalar(
                        out=ytm[:, qb % 4, hh * 64:(hh + 1) * 64], in0=pov[:, hh, 0:64], scalar1=ri[:, hh:hh + 1], scalar2=None,
                        op0=ALU.mult), reads=[bpo, bri], writes=[bytm])
                if qb % 4 == 3:
                    pt, bpt = self.psum(PSB)
                    ptb = pt[:].bitcast(BF16)
                    k.group("pe", [lambda e, j=j, ytm=ytm, ptb=ptb: e.transpose(
                        out=ptb[:, j * 128:(j + 1) * 128], in_=ytm[:, j, :], identity=ident) for j in range(4)],
                        reads=[bytm, self.b_cbf], writes=[bpt])
                    tt = qb // 4
                    k.op("act", lambda e, ptb=ptb, tt=tt: e.activation(
                        out=self.yT[:, slot * 4 + p, tt * 512:(tt + 1) * 512], in_=ptb[:, 0:512], func=AF.Copy),
                        reads=[bpt], writes=[self.b_yT[slot * 4 + p][tt]])
        loads = [
            (lambda w, i=i: w[:].rearrange("p (a b) -> p a b", a=8)[:, :, i * 128:(i + 1) * 128],
             self.w_in[l][:, off + p * 128:off + (p + 1) * 128].rearrange("(kc kp) n -> kp kc n", kp=128))
            for i, off in enumerate((O_BQ, O_BK, O_BV))]
        return (loads, fn)

    for p in range(4):
        steps.append(pair_step(p))

    def end(w, bw):
        self.arena_close()
    if not ext:
        steps.append(([], end))


Prog.mix_b = _mix_b


def _mix_d(self, l, steps, slot):
    k = self.k
    st = {}

    def begin(w, bw):
        self.arena_open()
        st["qT"] = self.at("d_qT", [128, SEQ], BF16)
        st["nqT"] = self.at("d_nqT", [128, SEQ], BF16)
        st["kTp"] = [self.at(f"d_kTp{h}", [128, SEQ], BF16) for h in range(2)]
        st["vtp"] = self.at("d_vtp", [128, 16, 2, 128], BF16)
        st["e"] = [self.at(f"d_e{i}", [128, 512], F32) for i in range(3)]
        st["lg"] = [self.at(f"d_lg{i}", [128, 512], BF16) for i in range(3)]
        st["tmp"] = [self.at(f"d_tmp{i}", [128, 512], F32) for i in range(3)]
        st["pT"] = [self.at(f"d_pT{i}", [128, 512], BF16) for i in range(3)]
        st["R"] = [self.at(f"d_R{i}", [128, 512], F32) for i in range(2)]
        st["cnt"] = 0
        st["qrc"] = 0
        for h in range(2):
            t, b = st["kTp"][h]
            o = (1 - h) * 64
            k.op("dve", lambda e, t=t, o=o: e.memset(t[o:o + 64, :], 0.0), writes=[b])
        vtp, bvtp = st["vtp"]
        k.op("dve", lambda e: e.memset(vtp[:], 0.0), writes=[bvtp])
    steps.append(([], begin))

    def pair_step(p):
        def fn(w, bw):
            k.sched_begin()
            fn_(w, bw)
            k.sched_end(("d", p))

        def fn_(w, bw):
            wv = w[:].rearrange("p (a b) -> p a b", a=8)
            qT, bqT = st["qT"]; nqT, bnqT = st["nqT"]; vtp, bvtp = st["vtp"]
            kTp = st["kTp"]
            uincl = self.cbf[:, C_UINCL:C_UINCL + 128]
            ones = self.cbf[:, C_ONES:C_ONES + 128]
            m01 = self.cbf[:, C_MASK01:C_MASK01 + 128]
            for tt in range(4):
                ts = slice(tt * 512, (tt + 1) * 512)
                ps, bps = self.psum(range(6))
                k.group("pe", [lambda e, kc=kc, ps=ps, ts=ts: e.matmul(
                    ps[:], lhsT=wv[:, kc, 0:128], rhs=self.xnT[:, kc, ts], start=(kc == 0), stop=(kc == 7)) for kc in range(8)],
                    reads=[bw, self.b_xn[tt]], writes=[bps])
                k.op("act", lambda e, ps=ps, ts=ts: e.activation(out=qT[:, ts], in_=ps[:], func=AF.Copy), reads=[bps], writes=[bqT])
                k.op("act", lambda e, ts=ts: e.mul(nqT[:, ts], qT[:, ts], -0.125), reads=[bqT], writes=[bnqT])
                ps, bps = self.psum(range(6))
                k.group("pe", [lambda e, kc=kc, ps=ps, ts=ts: e.matmul(
                    ps[:], lhsT=wv[:, kc, 128:256], rhs=self.xnT[:, kc, ts], start=(kc == 0), stop=(kc == 7)) for kc in range(8)],
                    reads=[bw, self.b_xn[tt]], writes=[bps])
                for h in range(2):
                    t, b = kTp[h]
                    k.op("dve", lambda e, ps=ps, ts=ts, t=t, h=h: e.tensor_copy(
                        out=t[h * 64:(h + 1) * 64, ts], in_=ps[h * 64:(h + 1) * 64, :]), reads=[bps], writes=[b])
                ps, bps = self.psum(range(6))
                fns = []
                for j in range(4):
                    tb = tt * 4 + j
                    for kc in range(8):
                        fns.append(lambda e, kc=kc, ps=ps, j=j, tb=tb: e.matmul(
                            ps[:, j * 128:(j + 1) * 128], lhsT=self.xnT[:, kc, tb * 128:(tb + 1) * 128], rhs=wv[:, kc, 256:384],
                            start=(kc == 0), stop=(kc == 7)))
                k.group("pe", fns, reads=[bw, self.b_xn[tt]], writes=[bps])
                for h in range(2):
                    k.op("dve", lambda e, ps=ps, tt=tt, h=h: e.tensor_copy(
                        out=vtp[:, tt * 4:(tt + 1) * 4, h, h * 64:(h + 1) * 64],
                        in_=ps[:].rearrange("p (a d) -> p a d", a=4)[:, :, h * 64:(h + 1) * 64]), reads=[bps], writes=[bvtp], n=256)

            for qr in range(self.flags.get("d_qr", 4)):
                qi = st["qrc"]; st["qrc"] += 1
                po, bpo = self.ps[6 + qi % 2], self.b_ps[6 + qi % 2]
                Rs = st["R"]
                for h in range(2):
                    k.op("dve", lambda e, h=h: e.memset(Rs[h][0][:], 0.0), writes=[Rs[h][1]])
                nkb = qr * 4 + 4
                first = [True]
                for kb in range(nkb - 1, -1, -1):
                    c0 = max(0, (kb - qr * 4) * 128)
                    diag = kb >= qr * 4
                    kbs = slice(kb * 128, (kb + 1) * 128)
                    qsl = slice(qr * 512 + c0, (qr + 1) * 512)
                    wn = 512 - c0
                    for h in range(2):
                        ci = st["cnt"]; st["cnt"] += 1
                        kt, bkt = kTp[h]
                        R, bR = Rs[h]
                        e_, be = st["e"][ci % 3]; lg, blg = st["lg"][ci % 3]
                        tmp, btmp = st["tmp"][ci % 3]; pT, bpT = st["pT"][ci % 3]
                        pss, bpss = self.psum(range(6))
                        k.op("pe", lambda e, pss=pss, kt=kt, kbs=kbs, qsl=qsl, c0=c0: e.matmul(
                            pss[:, c0:512], lhsT=kt[:, kbs], rhs=qT[:, qsl], start=True, stop=True),
                            reads=[bkt, bqT], writes=[bpss], n=wn)
                        k.op("act", lambda e, pss=pss, e_=e_, c0=c0: e.activation(
                            out=e_[:, c0:512], in_=pss[:, c0:512], func=AF.Exp, scale=0.125), reads=[bpss], writes=[be], n=wn)
                        k.op("act", lambda e, lg=lg, e_=e_, c0=c0: e.activation(
                            out=lg[:, c0:512], in_=e_[:, c0:512], func=AF.Ln, bias=1.0), reads=[be], writes=[blg], n=wn)
                        if diag:
                            k.op("dve", lambda e, lg=lg, c0=c0: e.tensor_tensor(
                                out=lg[:, c0:c0 + 128], in0=lg[:, c0:c0 + 128], in1=m01, op=ALU.mult),
                                reads=[blg, self.b_cbf], writes=[blg], n=128)
                        psw, bpsw = self.psum(range(6))
                        k.group("pe", [
                            lambda e, psw=psw, lg=lg, c0=c0: e.matmul(psw[:, c0:512], lhsT=uincl, rhs=lg[:, c0:512], start=True, stop=False),
                            lambda e, psw=psw, kt=kt, kbs=kbs, qsl=qsl, c0=c0: e.matmul(
                                psw[:, c0:512], lhsT=kt[:, kbs], rhs=nqT[:, qsl], start=False, stop=True)],
                            reads=[blg, self.b_cbf, bkt, bnqT], writes=[bpsw], n=wn)
                        k.op("dve", lambda e, tmp=tmp, psw=psw, R=R, c0=c0: e.tensor_tensor(
                            out=tmp[:, c0:512], in0=psw[:, c0:512], in1=R[:, c0:512], op=ALU.add), reads=[bpsw, bR], writes=[btmp], n=wn)
                        k.op("act", lambda e, tmp=tmp, pT=pT, c0=c0: e.activation(
                            out=pT[:, c0:512], in_=tmp[:, c0:512], func=AF.Exp, scale=-1.0), reads=[btmp], writes=[bpT], n=wn)
                        if diag:
                            k.op("dve", lambda e, pT=pT, c0=c0: e.tensor_tensor(
                                out=pT[:, c0:c0 + 128], in0=pT[:, c0:c0 + 128], in1=m01, op=ALU.mult),
                                reads=[bpT, self.b_cbf], writes=[bpT], n=128)
                        if kb > 0:
                            psc, bpsc = self.psum(range(6))
                            k.op("pe", lambda e, psc=psc, lg=lg, c0=c0: e.matmul(
                                psc[:, c0:512], lhsT=ones, rhs=lg[:, c0:512], start=True, stop=True),
                                reads=[blg, self.b_cbf], writes=[bpsc], n=wn)
                            k.op("dve", lambda e, psc=psc, R=R, c0=c0: e.tensor_tensor(
                                out=R[:, c0:512], in0=R[:, c0:512], in1=psc[:, c0:512], op=ALU.add), reads=[bpsc, bR], writes=[bR], n=wn)
                        is_first = first[0]; first[0] = False
                        is_last = (kb == 0 and h == 1)
                        k.op("pe", lambda e, pT=pT, kb=kb, h=h, c0=c0, is_first=is_first, is_last=is_last: e.matmul(
                            po[:, c0:512], lhsT=vtp[:, kb, h, :], rhs=pT[:, c0:512], start=is_first, stop=is_last,
                            skip_group_check=True), reads=[bpT, bvtp], writes=[bpo], n=wn)
                k.op("act", lambda e, qr=qr: e.activation(
                    out=self.yT[:, slot * 4 + p, qr * 512:(qr + 1) * 512], in_=po[:, :], func=AF.Copy),
                    reads=[bpo], writes=[self.b_yT[slot * 4 + p][qr]])
        loads = [
            (lambda w, i=i: w[:].rearrange("p (a b) -> p a b", a=8)[:, :, i * 128:(i + 1) * 128],
             self.w_in[l][:, off + p * 128:off + (p + 1) * 128].rearrange("(kc kp) n -> kp kc n", kp=128))
            for i, off in enumerate((O_DQ, O_DK, O_DV))]
        return (loads, fn)

    for p in range(self.flags.get("d_pairs", 4)):
        steps.append(pair_step(p))

    def end(w, bw):
        self.arena_close()
    steps.append(([], end))


Prog.mix_d = _mix_d


def _mix_a(self, l, steps, slot):
    k = self.k
    st = {}
    ROT = range(5)

    def begin(w, bw):
        self.arena_open()
        st["rT"] = self.at("a_rT", [16, SEQ], BF16)
        st["wgk"] = self.at("a_wgk", [16, 256], BF16)
        st["bgk"] = self.at("a_bgk", [128, 256], F32)
        st["ng"] = self.at("a_ng", [128, 128], F32)
        st["sg"] = self.at("a_sg", [128, 16, 256], BF16)
        st["qT"] = self.at("a_qT", [128, SEQ], BF16)
        st["kd"] = self.at("a_kd", [128, 16, 128], BF16)
        st["vtm"] = self.at("a_vtm", [128, 16, 256], BF16)
        st["xg"] = [self.at(f"a_xg{i}", [128, 128], F32) for i in range(2)]
        st["gkn"] = [self.at(f"a_gkn{i}", [128, 128], BF16) for i in range(2)]
        st["edec"] = [self.at(f"a_edec{i}", [128, 128], F32) for i in range(2)]
        st["etot"] = self.at("a_etot", [128, 32], F32)
        st["state"] = [self.at(f"a_state{i}", [128, 128], F32) for i in range(2)]
        st["sbf"] = [self.at(f"a_sbf{i}", [128, 256], BF16) for i in range(2)]
        st["ss"] = [self.at(f"a_ss{i}", [128, 2], F32) for i in range(2)]
        st["junk"] = self.at("a_junk", [128, 128], F32)
        st["tf"] = [self.at(f"a_tf{i}", [128, 256], F32) for i in range(2)]
        st["ytm"] = [self.at(f"a_ytm{i}", [128, 256], BF16) for i in range(2)]
        self.dma("pool", st["wgk"][0][:], self.a_w_gk[l], writes=[st["wgk"][1]])
        self.dma("sp", st["bgk"][0][:], self.a_bgk[l], writes=[st["bgk"][1]])
        self.dma("sp", st["ng"][0][:], self.a_ng[l], writes=[st["ng"][1]])
    steps.append(([], begin))

    def step0(p):
        def fn(w, bw):
            k.sched_begin()
            fn_(w, bw)
            k.sched_end(("a0", p))

        def fn_(w, bw):
            wv = w[:].rearrange("p (a b) -> p a b", a=8)
            rT, brT = st["rT"]; sg, bsg = st["sg"]
            if p == 0:
                for tt in range(4):
                    ts = slice(tt * 512, (tt + 1) * 512)
                    ps, bps = self.psum(ROT)
                    k.group("pe", [lambda e, kc=kc, ps=ps, ts=ts: e.matmul(
                        ps[0:16, :], lhsT=wv[:, kc, 256:272], rhs=self.xnT[:, kc, ts], start=(kc == 0), stop=(kc == 7))
                        for kc in range(8)], reads=[bw, self.b_xn[tt]], writes=[bps])
                    k.op("act", lambda e, ps=ps, ts=ts: e.activation(out=rT[0:16, ts], in_=ps[0:16, :], func=AF.Copy),
                         reads=[bps], writes=[brT])
            for t2 in range(8):
                ps, bps = self.psum(ROT)
                fns = []
                for j in range(2):
                    tb = t2 * 2 + j
                    for kc in range(8):
                        fns.append(lambda e, kc=kc, ps=ps, j=j, tb=tb: e.matmul(
                            ps[:, j * 256:(j + 1) * 256], lhsT=self.xnT[:, kc, tb * 128:(tb + 1) * 128], rhs=wv[:, kc, 0:256],
                            start=(kc == 0), stop=(kc == 7)))
                k.group("pe", fns, reads=[bw, self.b_xn[t2 // 2]], writes=[bps])
                k.op("act", lambda e, ps=ps, t2=t2: e.activation(
                    out=sg[:, t2 * 2:t2 * 2 + 2, :], in_=ps[:].rearrange("p (a b) -> p a b", a=2), func=AF.Silu),
                    reads=[bps], writes=[bsg])
        loads = [
            (lambda w: w[:].rearrange("p (a b) -> p a b", a=8)[:, :, 0:256],
             self.w_in[l][:, O_AG + p * 256:O_AG + (p + 1) * 256].rearrange("(kc kp) n -> kp kc n", kp=128)),
            (lambda w: w[:].rearrange("p (a b) -> p a b", a=8)[:, :, 256:272],
             self.w_in[l][:, O_AR:O_AR + 16].rearrange("(kc kp) n -> kp kc n", kp=128)),
        ]
        return (loads, fn)

    def step1(p):
        def fn(w, bw):
            k.sched_begin()
            fn_(w, bw)
            k.sched_end(("a1", p))

        def fn_(w, bw):
            wv = w[:].rearrange("p (a b) -> p a b", a=8)
            rT, brT = st["rT"]; sg, bsg = st["sg"]; wgk, bwgk = st["wgk"]; bgk, bbgk = st["bgk"]; ng, bng = st["ng"]
            qT, bqT = st["qT"]; kd, bkd = st["kd"]; vtm, bvtm = st["vtm"]
            etot, betot = st["etot"]; junk, bjunk = st["junk"]
            su = self.cbf[:, C_SU:C_SU + 128]
            ch = self.cbf[:, C_CH:C_CH + 2]
            ident = self.cbf[:, C_IDENT:C_IDENT + 128]
            ptot, bptot = self.ps[7], self.b_ps[7]
            for tt in range(4):
                ts = slice(tt * 512, (tt + 1) * 512)
                ps, bps = self.psum(ROT)
                k.group("pe", [lambda e, kc=kc, ps=ps, ts=ts: e.matmul(
                    ps[:], lhsT=wv[:, kc, 0:128], rhs=self.xnT[:, kc, ts], start=(kc == 0), stop=(kc == 7)) for kc in range(8)],
                    reads=[bw, self.b_xn[tt]], writes=[bps])
                k.op("act", lambda e, ps=ps, ts=ts: e.activation(out=qT[:, ts], in_=ps[:], func=AF.Identity, scale=0.125),
                     reads=[bps], writes=[bqT])
            for tb in range(16):
                tbs = slice(tb * 128, (tb + 1) * 128)
                xg, bxg = st["xg"][tb % 2]; gkn, bgkn = st["gkn"][tb % 2]; edec, bedec = st["edec"][tb % 2]
                pg, bpg = self.psum(ROT)
                k.op("pe", lambda e, pg=pg, tbs=tbs: e.matmul(
                    pg[:, 0:128], lhsT=rT[0:16, tbs], rhs=wgk[0:16, p * 128:(p + 1) * 128], start=True, stop=True),
                    reads=[brT, bwgk], writes=[bpg])
                k.op("dve", lambda e, pg=pg, xg=xg: e.tensor_tensor(
                    out=xg[:], in0=pg[:, 0:128], in1=bgk[:, p * 128:(p + 1) * 128], op=ALU.add), reads=[bpg, bbgk], writes=[bxg])
                k.op("act", lambda e, xg=xg: e.activation(out=xg[:], in_=xg[:], func=AF.Exp, scale=-1.0), reads=[bxg], writes=[bxg])
                k.op("act", lambda e, xg=xg, gkn=gkn: e.activation(out=gkn[:], in_=xg[:], func=AF.Ln, bias=1.0),
                     reads=[bxg], writes=[bgkn])
                pd, bpd = self.psum(ROT)
                k.op("pe", lambda e, pd=pd, gkn=gkn: e.matmul(pd[:, 0:128], lhsT=su, rhs=gkn[:], start=True, stop=True),
                     reads=[bgkn, self.b_cbf], writes=[bpd])
                k.op("act", lambda e, pd=pd, edec=edec: e.activation(out=edec[:], in_=pd[:, 0:128], func=AF.Exp, scale=-1.0 / 16.0),
                     reads=[bpd], writes=[bedec])
                k.op("pe", lambda e, gkn=gkn, tb=tb: e.matmul(ptot[:, 2 * tb:2 * tb + 2], lhsT=gkn[:], rhs=ch, start=True, stop=True),
                     reads=[bgkn, self.b_cbf], writes=[bptot])
                pkv, bpkv = self.psum(ROT)
                k.group("pe", [lambda e, kc=kc, pkv=pkv, tbs=tbs: e.matmul(
                    pkv[:, 0:384], lhsT=self.xnT[:, kc, tbs], rhs=wv[:, kc, 128:512], start=(kc == 0), stop=(kc == 7))
                    for kc in range(8)], reads=[bw, self.b_xn[tb // 4]], writes=[bpkv])
                k.op("dve", lambda e, pkv=pkv, edec=edec, tb=tb: e.tensor_tensor(
                    out=kd[:, tb, :], in0=pkv[:, 0:128], in1=edec[:], op=ALU.mult), reads=[bpkv, bedec], writes=[bkd])
                k.op("dve", lambda e, pkv=pkv, tb=tb: e.tensor_copy(out=vtm[:, tb, :], in_=pkv[:, 128:384]),
                     reads=[bpkv], writes=[bvtm])
            k.op("act", lambda e: e.activation(out=etot[:], in_=ptot[:, 0:32], func=AF.Exp, scale=-1.0 / 16.0),
                 reads=[bptot], writes=[betot])
            k.op("dve", lambda e: e.memset(st["state"][1][0][:], 0.0), writes=[st["state"][1][1]], n=128)
            for i in range(2):
                k.op("pool", lambda e, i=i: e.memset(st["sbf"][i][0][:], 0.0), writes=[st["sbf"][i][1]])

            S = {}

            def s0(c):
                tb, half = c // 2, c % 2
                rows = slice(half * 64, (half + 1) * 64)
                pk, bpk = self.psum(ROT)
                k.group("pe", [lambda e, hh=hh, pk=pk, tb=tb, rows=rows: e.matmul(
                    pk[hh * 64:(hh + 1) * 64, 0:128], lhsT=kd[rows, tb, hh * 64:(hh + 1) * 64], rhs=vtm[rows, tb, hh * 128:(hh + 1) * 128],
                    start=True, stop=True, skip_group_check=True) for hh in range(2)], reads=[bkd, bvtm], writes=[bpk])
                S[c] = (pk, bpk)

            def s1(c):
                pk, bpk = S[c]
                sbf, bsbf = st["sbf"][c % 2]
                sprev, bsprev = st["state"][(c + 1) % 2]
                state, bstate = st["state"][c % 2]
                k.op("dve", lambda e, pk=pk, c=c, sprev=sprev, state=state: e.scalar_tensor_tensor(
                    out=state[:], in0=sprev[:], scalar=etot[:, c:c + 1], in1=pk[:, 0:128], op0=ALU.mult, op1=ALU.add),
                    reads=[bpk, betot, bsprev], writes=[bstate], n=128)
                for hh in range(2):
                    eng = "act" if hh == 0 else "pool"
                    if eng == "act":
                        k.op("act", lambda e, sbf=sbf, hh=hh, state=state: e.activation(
                            out=sbf[hh * 64:(hh + 1) * 64, hh * 128:(hh + 1) * 128], in_=state[hh * 64:(hh + 1) * 64, :], func=AF.Copy),
                            reads=[bstate], writes=[bsbf], n=128)
                    else:
                        k.op("pool", lambda e, sbf=sbf, hh=hh, state=state: e.tensor_copy(
                            out=sbf[hh * 64:(hh + 1) * 64, hh * 128:(hh + 1) * 128], in_=state[hh * 64:(hh + 1) * 64, :]),
                            reads=[bstate], writes=[bsbf], n=128)
                S[c] = (sbf, bsbf)

            def s2(c):
                sbf, bsbf = S[c]
                tb, half = c // 2, c % 2
                rows = slice(half * 64, (half + 1) * 64)
                po, bpo = self.ps[5 + tb % 2], self.b_ps[5 + tb % 2]
                k.op("pe", lambda e, po=po, rows=rows, c=c, sbf=sbf: e.matmul(
                    po[rows, 0:256], lhsT=qT[:, c * 64:(c + 1) * 64], rhs=sbf[:, 0:256],
                    start=True, stop=True, skip_group_check=True), reads=[bqT, bsbf], writes=[bpo])
                if half == 1:
                    ss, bss = st["ss"][tb % 2]; tf, btf = st["tf"][tb % 2]; ytm, bytm = st["ytm"][tb % 2]
                    for hh in range(2):
                        k.op("act", lambda e, hh=hh, po=po, ss=ss: e.activation(
                            out=junk[:], in_=po[:, hh * 128:(hh + 1) * 128], func=AF.Square, accum_out=ss[:, hh:hh + 1]),
                            reads=[bpo], writes=[bjunk, bss])
                    k.op("act", lambda e, ss=ss: e.activation(out=ss[:], in_=ss[:], func=AF.Ln, scale=1.0 / 128.0, bias=EPS),
                         reads=[bss], writes=[bss])
                    k.op("act", lambda e, ss=ss: e.activation(out=ss[:], in_=ss[:], func=AF.Exp, scale=-0.5), reads=[bss], writes=[bss])
                    for hh in range(2):
                        k.op("dve", lambda e, hh=hh, po=po, ss=ss, tf=tf: e.scalar_tensor_tensor(
                            out=tf[:, hh * 128:(hh + 1) * 128], in0=po[:, hh * 128:(hh + 1) * 128], scalar=ss[:, hh:hh + 1], in1=ng[:],
                            op0=ALU.mult, op1=ALU.mult), reads=[bpo, bss, bng], writes=[btf])
                    k.op("pool", lambda e, tf=tf, ytm=ytm, tb=tb: e.tensor_tensor(out=ytm[:], in0=tf[:], in1=sg[:, tb, :], op=ALU.mult),
                         reads=[btf, bsg], writes=[bytm])
                    pt, bpt = self.psum(ROT)
                    ptb = pt[:].bitcast(BF16)
                    k.group("pe", [lambda e, j=j, ytm=ytm, ptb=ptb: e.transpose(
                        out=ptb[:, j * 128:(j + 1) * 128], in_=ytm[:, j * 128:(j + 1) * 128], identity=ident) for j in range(2)],
                        reads=[bytm, self.b_cbf], writes=[bpt])
                    c0 = slot * 4 + p * 2
                    k.op("act", lambda e, ptb=ptb, tb=tb, c0=c0: e.activation(
                        out=self.yT[:, c0:c0 + 2, tb * 128:(tb + 1) * 128], in_=ptb[:, 0:256].rearrange("p (a b) -> p a b", a=2),
                        func=AF.Copy), reads=[bpt], writes=[self.b_yT[c0][tb // 4], self.b_yT[c0 + 1][tb // 4]])

            pipeline(32, [s0, s1, s2])
        loads = [
            (lambda w: w[:].rearrange("p (a b) -> p a b", a=8)[:, :, 0:128],
             self.w_in[l][:, O_AQ + p * 128:O_AQ + (p + 1) * 128].rearrange("(kc kp) n -> kp kc n", kp=128)),
            (lambda w: w[:].rearrange("p (a b) -> p a b", a=8)[:, :, 128:256],
             self.w_in[l][:, O_AK + p * 128:O_AK + (p + 1) * 128].rearrange("(kc kp) n -> kp kc n", kp=128)),
            (lambda w: w[:].rearrange("p (a b) -> p a b", a=8)[:, :, 256:512],
             self.w_in[l][:, O_AV + p * 256:O_AV + (p + 1) * 256].rearrange("(kc kp) n -> kp kc n", kp=128)),
        ]
        return (loads, fn)

    for p in range(2):
        steps.append(step0(p))
        steps.append(step1(p))

    def end(w, bw):
        self.arena_close()
    steps.append(([], end))


Prog.mix_a = _mix_a
```

```python
import types
import numpy as np
from contextlib import ExitStack
import concourse.bass as bass
import concourse.mybir as mybir
from concourse.bass_utils import run_bass_kernel_spmd

F32 = mybir.dt.float32
BF16 = mybir.dt.bfloat16
AF = mybir.ActivationFunctionType
ALU = mybir.AluOpType
AX = mybir.AxisListType

D_MODEL = 1024
SEQ = 2048
DEPTH = 4
NCORES = 8
EPS = 1e-6
FFN_HIDDEN = 2816
IN_SIZES = (256, 256, 512, 16, 512, 512, 512, 512, 512, 512, 512, 512, 512, 4096)
IN_OFF = [int(v) for v in np.cumsum((0,) + IN_SIZES)]
(O_AQ, O_AK, O_AV, O_AR, O_AG, O_BQ, O_BK, O_BV, O_CG, O_CX, O_DQ, O_DK, O_DV, O_GATE) = IN_OFF[:14]
IN_TOTAL = IN_OFF[14]


class Buf:
    __slots__ = ("name", "last_w", "readers", "dsem", "dcnt")

    def __init__(self, name):
        self.name = name
        self.last_w = None
        self.readers = []
        self.dsem = None
        self.dcnt = 0


def freeze(fn):
    if fn.__closure__ is None:
        return fn
    cells = []
    for c in fn.__closure__:
        try:
            cells.append(types.CellType(c.cell_contents))
        except ValueError:
            cells.append(c)
    return types.FunctionType(fn.__code__, fn.__globals__, fn.__name__, fn.__defaults__, tuple(cells))


class K:
    ENG = ("pe", "act", "dve", "pool", "sp")

    def __init__(self, nc, stack):
        self.nc = nc
        self.stack = stack
        self.eng = {"pe": nc.tensor, "act": nc.scalar, "dve": nc.vector, "pool": nc.gpsimd, "sp": nc.sync}
        self.sem = {}
        self.cnt = {}
        for e in self.ENG:
            self.sem[e] = stack.enter_context(nc.semaphore("s_" + e))
            self.cnt[e] = 0
        self.seen = {e: {} for e in self.ENG}
        self.pend = None
        self.sched_cache = {}
        self.prog = {e: [] for e in self.ENG}
        self.n_inst = 0
        self.n_wait = 0

    def _sem_of(self, key):
        if isinstance(key, str):
            return self.sem[key]
        return key.dsem

    def _wait(self, on, tokens):
        need = {}
        for tok in tokens:
            if tok is None:
                continue
            key, val = tok
            if on == "pe" and key == "pe":
                continue
            if self.seen[on].get(key, 0) >= val:
                continue
            if need.get(key, 0) < val:
                need[key] = val
        for key, val in need.items():
            self.prog[on].append(("w", self._sem_of(key), val))
            self.seen[on][key] = val
            self.n_wait += 1

    def deps(self, reads, writes):
        toks = []
        for b in reads:
            toks.append(b.last_w)
        for b in writes:
            toks.append(b.last_w)
            toks.extend(b.readers)
        return toks

    def commit(self, tok, reads, writes):
        for b in reads:
            b.readers.append(tok)
            if len(b.readers) > 24:
                best = {}
                for k_, v_ in b.readers:
                    if best.get(k_, 0) < v_:
                        best[k_] = v_
                b.readers = list(best.items())
        for b in writes:
            b.last_w = tok
            b.readers = []

    def op(self, on, fn, reads=(), writes=(), extra=(), n=512, tab=None):
        if self.pend is not None:
            self.pend.append(("op", on, freeze(fn), tuple(reads), tuple(writes), self.est(on, n, 1), tab))
            return None
        self._wait(on, self.deps(reads, writes) + list(extra))
        self.cnt[on] += 1
        self.prog[on].append(("i", freeze(fn), self.sem[on], 1))
        tok = (on, self.cnt[on])
        self.commit(tok, reads, writes)
        self.n_inst += 1
        return tok

    def group(self, on, fns, reads=(), writes=(), n=512):
        if self.pend is not None:
            self.pend.append(("group", on, [freeze(f) for f in fns], tuple(reads), tuple(writes), self.est(on, n, len(fns)), None))
            return None
        self._wait(on, self.deps(reads, writes))
        for fn in fns[:-1]:
            self.prog[on].append(("i", freeze(fn), None, 0))
            self.n_inst += 1
        self.n_inst += 1
        self.cnt[on] += 1
        self.prog[on].append(("i", freeze(fns[-1]), self.sem[on], 1))
        tok = (on, self.cnt[on])
        self.commit(tok, reads, writes)
        return tok

    def dma(self, on, out, in_, reads=(), writes=(), sembuf=None):
        sb = sembuf or (writes[0] if writes else reads[0])
        if self.pend is not None:
            self.pend.append(("dma", on, (out, in_, sb), tuple(reads), tuple(writes), 2.0, None))
            return None
        if sb.dsem is None:
            kind = "sw" if on == "pool" else "hw"
            free = getattr(self, "free_dsems", {}).get(kind)
            sb.name = kind + ":" + sb.name
            if free:
                sb.dsem, sb.dcnt = free.pop()
            else:
                self._nsem = getattr(self, "_nsem", 0) + 1
                sb.dsem = self.stack.enter_context(self.nc.semaphore(f"d{self._nsem}_" + sb.name))
        self._wait(on, self.deps(reads, writes))
        sb.dcnt += 16
        self.prog[on].append(("i", (lambda e, o=out, i=in_: e.dma_start(out=o, in_=i)), sb.dsem, 16))
        tok = (sb, sb.dcnt)
        self.commit(tok, reads, writes)
        self.n_inst += 1
        return tok

    @staticmethod
    def est(on, n, cnt):
        if on == "pe":
            return cnt * (0.035 + n / 2400.0)
        if on == "act":
            return cnt * (0.22 + n / 1400.0)
        if on == "dve":
            return cnt * (0.12 + n / 960.0)
        if on == "pool":
            return cnt * (0.25 + n / 900.0)
        return 2.0

    def sched_begin(self):
        assert self.pend is None
        self.pend = []

    def sched_end(self, key=None):
        pend, self.pend = self.pend, None
        preds = self._dag(pend)
        order = self.sched_cache.get(key) if key is not None else None
        if order is not None:
            ok = len(order) == len(pend)
            if ok:
                pos = [0] * len(pend)
                for j, i in enumerate(order):
                    pos[i] = j
                ok = all(pos[p] < pos[i] for i in range(len(pend)) for p in preds[i])
            if not ok:
                order = None
        if order is None:
            order = self._schedule(pend, preds)
            if key is not None:
                self.sched_cache[key] = order
        for i in order:
            kind, on, payload, reads, writes, dur, tab = pend[i]
            if kind == "op":
                self.op(on, payload, reads=reads, writes=writes)
            elif kind == "group":
                self.group(on, payload, reads=reads, writes=writes)
            else:
                out, in_, sb = payload
                self.dma(on, out, in_, reads=reads, writes=writes, sembuf=sb)

    def _dag(self, pend):
        n = len(pend)
        preds = [set() for _ in range(n)]
        last_w, readers = {}, {}
        for i, (kind, on, payload, reads, writes, dur, tab) in enumerate(pend):
            for b in reads:
                w = last_w.get(id(b))
                if w is not None:
                    preds[i].add(w)
            for b in writes:
                w = last_w.get(id(b))
                if w is not None:
                    preds[i].add(w)
                for r in readers.get(id(b), ()):
                    preds[i].add(r)
            for b in reads:
                readers.setdefault(id(b), []).append(i)
            for b in writes:
                last_w[id(b)] = i
                readers[id(b)] = []
            preds[i].discard(i)
        return preds

    def _schedule(self, pend, preds):
        import heapq
        n = len(pend)
        succs = [[] for _ in range(n)]
        for i in range(n):
            for p in preds[i]:
                succs[p].append(i)
        prio = [0.0] * n
        for i in range(n - 1, -1, -1):
            m = 0.0
            for s in succs[i]:
                if prio[s] > m:
                    m = prio[s]
            prio[i] = pend[i][5] + m
        npred = [len(preds[i]) for i in range(n)]
        ready_t = [0.0] * n
        finish = [0.0] * n
        engs = list(self.ENG)
        fut = {e: [] for e in engs}
        avail = {e: [] for e in engs}
        free = {e: 0.0 for e in engs}
        for i in range(n):
            if npred[i] == 0:
                heapq.heappush(fut[pend[i][1]], (0.0, i))
        order = []
        done = 0
        cur_tab = [None]
        while done < n:
            best = None
            for e in engs:
                t = free[e]
                while fut[e] and fut[e][0][0] <= t:
                    rt, i = heapq.heappop(fut[e])
                    heapq.heappush(avail[e], (-prio[i], i))
                if avail[e]:
                    cand = (t, e)
                elif fut[e]:
                    cand = (fut[e][0][0], e)
                else:
                    continue
                if best is None or cand < best:
                    best = cand
            t, e = best
            if not avail[e]:
                rt, i = heapq.heappop(fut[e])
                heapq.heappush(avail[e], (-prio[i], i))
                while fut[e] and fut[e][0][0] <= t:
                    rt2, i2 = heapq.heappop(fut[e])
                    heapq.heappush(avail[e], (-prio[i2], i2))
            _, i = heapq.heappop(avail[e])
            tsw = 0.0
            if e == "act":
                if pend[i][6] is not None and pend[i][6] != cur_tab[0]:
                    held = [(-prio[i], i)]
                    found = None
                    while avail[e] and len(held) < 12:
                        pj, j = heapq.heappop(avail[e])
                        if pend[j][6] is None or pend[j][6] == cur_tab[0]:
                            found = j
                            break
                        held.append((pj, j))
                    if found is not None:
                        i = found
                        for it in held:
                            heapq.heappush(avail[e], it)
                    else:
                        for it in held[1:]:
                            heapq.heappush(avail[e], it)
                if pend[i][6] is not None and pend[i][6] != cur_tab[0]:
                    cur_tab[0] = pend[i][6]
                    tsw = 1.3
            start = max(free[e], ready_t[i]) + tsw
            finish[i] = start + pend[i][5]
            free[e] = finish[i]
            order.append(i)
            done += 1
            for s in succs[i]:
                lat = 0.08 if pend[s][1] == e else 0.2
                if finish[i] + lat > ready_t[s]:
                    ready_t[s] = finish[i] + lat
                npred[s] -= 1
                if npred[s] == 0:
                    heapq.heappush(fut[pend[s][1]], (ready_t[s], s))
        self.last_makespan = max(finish) if n else 0.0
        return order

    def finish(self, bufs):
        toks = []
        for b in bufs:
            toks.append(b.last_w)
            toks.extend(b.readers)
        self._wait("sp", toks)

    def emit(self):
        with self.nc.Block() as block:
            def make(name):
                def run(e):
                    for it in self.prog[name]:
                        if it[0] == "w":
                            e.wait_ge(it[1], it[2])
                        else:
                            ins = it[1](e)
                            if it[2] is not None:
                                ins.then_inc(it[2], it[3])
                return run
            block.tensor(make("pe"))
            block.scalar(make("act"))
            block.vector(make("dve"))
            block.gpsimd(make("pool"))
            block.sync(make("sp"))

    def barrier(self):
        toks = [(e, self.cnt[e]) for e in self.ENG if self.cnt[e] > 0]
        toks += [(b, b.dcnt) for b in self.dma_bufs]
        for e in self.ENG:
            self._wait(e, toks)


C_IDENT = 0
C_ONES = 128
C_UINCL = 256
C_MASK01 = 384
C_SU = 512
C_CH = 640
NCONST = 642


def make_consts():
    c = np.zeros((128, NCONST), np.float32)
    c[:, C_IDENT:C_IDENT + 128] = np.eye(128, dtype=np.float32)
    c[:, C_ONES:C_ONES + 128] = 1.0
    i = np.arange(128)
    c[:, C_UINCL:C_UINCL + 128] = (i[:, None] >= i[None, :])
    c[:, C_MASK01:C_MASK01 + 128] = (i[:, None] < i[None, :])
    c[:, C_SU:C_SU + 128] = (i[:, None] > i[None, :]) & ((i[:, None] // 64) == (i[None, :] // 64))
    c[:, C_CH:C_CH + 2] = ((i[:, None] // 64) == np.arange(2)[None, :])
    return c


NW = 3


class Prog:
    def __init__(self, nc, st, n_layers, dbg=None, flags=None):
        self.nc, self.st, self.L = nc, st, n_layers
        self.dbg = dbg or {}
        self.flags = flags or {}
        self.k = K(nc, st)
        self.k.dma_bufs = []
        self._declare_io()
        self._alloc()

    def sb(self, name, shape, dt, st=None):
        self._uid = getattr(self, "_uid", 0) + 1
        t = (st or self.st).enter_context(self.nc.sbuf_tensor(f"{name}_u{self._uid}", shape, dt))
        return t

    def buf(self, name):
        return Buf(name)

    def dma(self, on, out, in_, reads=(), writes=(), sembuf=None):
        sb = sembuf or (writes[0] if writes else reads[0])
        if sb not in self.k.dma_bufs:
            self.k.dma_bufs.append(sb)
        return self.k.dma(on, out, in_, reads=reads, writes=writes, sembuf=sembuf)

    def _declare_io(self):
        nc, L = self.nc, self.L
        def din(name, shape, dt=F32):
            return nc.dram_tensor(name, list(shape), dt, kind="ExternalInput").ap()
        self.x = din("x", [SEQ, D_MODEL])
        self.consts = din("consts", [128, NCONST])
        self.norm_mix = din("norm_mix", [L, 128, 8])
        self.norm_ffn = din("norm_ffn", [L, 128, 8])
        self.norm_final = din("norm_final", [128, 8])
        self.w_in = din("w_in", [L, D_MODEL, IN_TOTAL])
        self.w_branch = din("w_branch", [L, 4, 512, D_MODEL])
        self.cpar = din("cpar", [L, 128, 4, 8])
        self.c_w_a = din("c_w_a", [L, 8, 64, 64])
        self.c_w_x = din("c_w_x", [L, 8, 64, 64])
        self.b_biasT = din("b_biasT", [L, 128, 8, 5, 128])
        self.maskb = din("maskb", [128, 5, 128])
        self.a_w_gk = din("a_w_gk", [L, 16, 256])
        self.a_bgk = din("a_bgk", [L, 128, 256])
        self.a_ng = din("a_ng", [L, 128, 128])
        self.w_out = din("w_out", [L, D_MODEL, D_MODEL])
        self.w_fg = din("w_ffn_gate", [L, D_MODEL, FFN_HIDDEN])
        self.w_fu = din("w_ffn_up", [L, D_MODEL, FFN_HIDDEN])
        self.w_fd = din("w_ffn_down", [L, FFN_HIDDEN, D_MODEL])
        self.out = nc.dram_tensor("out", [SEQ, D_MODEL], F32, kind="ExternalOutput").ap()
        self.out_bufs = []
        self.dbg_out = {}
        self.dbg_bufs = []
        for name, shape in self.dbg.items():
            self.dbg_out[name] = nc.dram_tensor("dbg_" + name, list(shape), F32, kind="ExternalOutput").ap()

    def _alloc(self):
        nc = self.nc
        self.hT = self.sb("hT", [128, 8, SEQ], F32)
        self.b_hT = [[Buf(f"hT{c}_{t}") for t in range(4)] for c in range(8)]
        self.xnT = self.sb("xnT", [128, 8, SEQ], BF16)
        self.b_xn = [Buf(f"xn{t}") for t in range(4)]
        self.yT = self.sb("yT", [128, 8, SEQ], BF16)
        self.b_yT = [[Buf(f"yT{c}_{t}") for t in range(4)] for c in range(8)]
        self.cst = self.sb("cst", [128, NCONST], F32)
        self.b_cst = Buf("cst")
        self.cbf = self.sb("cbf", [128, NCONST], BF16)
        self.b_cbf = Buf("cbf")
        self.nrm = self.sb("nrm", [128, 3, 8], F32)
        self.b_nrm = Buf("nrm")
        self.wb = [self.sb(f"wb{i}", [128, 4096], BF16) for i in range(NW)]
        self.b_wb = [Buf(f"wb{i}") for i in range(NW)]
        self.wb_i = 0
        self.ps = [self.st.enter_context(nc.psum_tensor(f"ps{i}", [128, 512], F32)) for i in range(8)]
        self.b_ps = [Buf(f"ps{i}") for i in range(8)]
        self.ps_i = 0
        self.k.op("pool", lambda e: e.memset(self.nrm[:], 0.0), writes=[self.b_nrm])

    def psum(self, allowed=range(8)):
        allowed = list(allowed)
        while self.ps_i % 8 not in allowed:
            self.ps_i += 1
        i = self.ps_i % 8
        self.ps_i += 1
        return self.ps[i], self.b_ps[i]

    def run_steps(self, steps, depth=2):
        wsteps = [i for i, s in enumerate(steps) if s[0]]
        assigned = {}
        inflight = {}
        state = {"ptr": 0}

        pinmap = {}

        def try_issue():
            while state["ptr"] < len(wsteps) and len(inflight) < depth:
                si = wsteps[state["ptr"]]
                if len(steps[si]) > 2 and steps[si][2] is not None:
                    reg, li = steps[si][2]
                    if reg not in pinmap:
                        pinmap[reg] = self.wb_i
                        self.wb_i += 2
                    wi = (pinmap[reg] + li) % NW
                else:
                    wi = self.wb_i % NW
                if wi in inflight:
                    break
                if not (len(steps[si]) > 2 and steps[si][2] is not None):
                    self.wb_i += 1
                for dst_fn, src in steps[si][0]:
                    self.dma("pool", dst_fn(self.wb[wi]), src, writes=[self.b_wb[wi]])
                assigned[si] = wi
                inflight[wi] = si
                state["ptr"] += 1

        try_issue()
        for i, s in enumerate(steps):
            if s[0]:
                if i not in assigned:
                    try_issue()
                assert i in assigned, "weight tile not issued"
                wi = assigned[i]
                s[1](self.wb[wi], self.b_wb[wi])
                del inflight[wi]
                try_issue()
            else:
                s[1](None, None)

    def arena_open(self):
        self.ar = ExitStack()
        self.ar_bufs = []

    def arena_close(self, hard=False):
        if not hasattr(self.k, "free_dsems"):
            self.k.free_dsems = {"sw": [], "hw": []}
        if hard or not self.flags.get("soft_arena", True):
            self.k.barrier()
            self.gen_tokens = []
        else:
            best = {}
            for b in self.ar_bufs:
                for tok in [b.last_w] + list(b.readers):
                    if tok is None:
                        continue
                    key, val = tok
                    if best.get(key, 0) < val:
                        best[key] = val
            self.gen_tokens = list(best.items())
        for b in self.ar_bufs:
            if b.dsem is not None:
                self.k.free_dsems[b.name[:2]].append((b.dsem, b.dcnt))
                if b in self.k.dma_bufs:
                    self.k.dma_bufs.remove(b)
        self.ar.close()
        self.ar = None

    def abuf(self, name):
        b = Buf(name)
        b.readers = list(getattr(self, "gen_tokens", []))
        self.ar_bufs.append(b)
        return b

    def at(self, name, shape, dt):
        b = self.abuf(name)
        return self.sb(name, shape, dt, st=self.ar), b

    def ph_load(self):
        k = self.k
        self.dma("sp", self.cst[:], self.consts[:, :], writes=[self.b_cst])
        self.dma("pool", self.cbf[:], self.consts[:, :], writes=[self.b_cbf])
        self.dma("sp", self.nrm[:, 2, :], self.norm_final[:, :], writes=[self.b_nrm])
        self.arena_open()
        stg = [self.at(f"xstg{i}", [128, D_MODEL], F32) for i in range(4)]
        ident = self.cst[:, C_IDENT:C_IDENT + 128]
        for tb in range(16):
            t, b = stg[tb % 4]
            self.dma("sp", t[:], self.x[tb * 128:(tb + 1) * 128, :], writes=[b])
            for half in range(2):
                ps, bps = self.psum()
                fns = []
                for j in range(4):
                    c = half * 4 + j
                    fns.append(lambda e, ps=ps, t=t, c=c, j=j: e.transpose(
                        out=ps[:, j * 128:(j + 1) * 128], in_=t[:, c * 128:(c + 1) * 128], identity=ident))
                k.group("pe", fns, reads=[b, self.b_cst], writes=[bps])
                tt = tb // 4
                dst = self.hT[:, half * 4:half * 4 + 4, tb * 128:(tb + 1) * 128]
                wr = [self.b_hT[half * 4 + j][tt] for j in range(4)]
                eng = "dve" if half == 0 else "act"
                if eng == "dve":
                    k.op("dve", lambda e, ps=ps, dst=dst: e.tensor_copy(
                        out=dst, in_=ps[:].rearrange("p (a b) -> p a b", a=4)), reads=[bps], writes=wr)
                else:
                    k.op("act", lambda e, ps=ps, dst=dst: e.activation(
                        out=dst, in_=ps[:].rearrange("p (a b) -> p a b", a=4), func=AF.Copy), reads=[bps], writes=wr)
        self.arena_close()

    def ph_norm(self, gsel, l):
        k = self.k
        if gsel < 2:
            src = (self.norm_mix if gsel == 0 else self.norm_ffn)[l]
            self.dma("sp", self.nrm[:, gsel, :], src, writes=[self.b_nrm])
        self.arena_open()
        sq = [self.at(f"sq{i}", [128, 512], BF16) for i in range(3)]
        lnb = self.at("lnb", [128, 512], F32)
        rstd = [self.at(f"rstd{i}", [128, 512], F32) for i in range(2)]
        ones = self.cbf[:, C_ONES:C_ONES + 128]
        for tt in range(4):
            ts = slice(tt * 512, (tt + 1) * 512)
            ps, bps = self.psum()
            for c in range(8):
                s, bs = sq[c % 3]
                k.op("act", lambda e, s=s, c=c, ts=ts: e.activation(out=s[:], in_=self.hT[:, c, ts], func=AF.Square),
                     reads=[self.b_hT[c][tt]], writes=[bs])
                k.op("pe", lambda e, ps=ps, s=s, c=c: e.matmul(ps[:], lhsT=ones, rhs=s[:], start=(c == 0), stop=(c == 7)),
                     reads=[bs, self.b_cbf], writes=[bps])
            r, br = rstd[tt % 2]
            k.op("act", lambda e, ps=ps: e.activation(out=lnb[0][:], in_=ps[:], func=AF.Ln, scale=1.0 / D_MODEL, bias=EPS),
                 reads=[bps], writes=[lnb[1]])
            k.op("act", lambda e, r=r: e.activation(out=r[:], in_=lnb[0][:], func=AF.Exp, scale=-0.5),
                 reads=[lnb[1]], writes=[br])
            for c in range(8):
                k.op("dve", lambda e, c=c, ts=ts, r=r: e.scalar_tensor_tensor(
                    out=self.xnT[:, c, ts], in0=self.hT[:, c, ts], scalar=self.nrm[:, gsel, c:c + 1], in1=r[:],
                    op0=ALU.mult, op1=ALU.mult), reads=[self.b_hT[c][tt], br, self.b_nrm], writes=[self.b_xn[tt]])
        self.arena_close()

    def ph_ffn(self, l, steps):
        k = self.k
        NJ = FFN_HIDDEN // 128
        groups = [(0, 8), (8, 16), (16, 22)]
        st = {}

        def begin(w, bw):
            self.arena_open()
            st["act"] = self.sb("ffn_act", [128, 8, SEQ], BF16, st=self.ar)
            st["b_act"] = [[self.abuf(f"act{j}_{t}") for t in range(4)] for j in range(8)]
            st["sg"] = [self.at(f"ffn_sg{i}", [128, 512], F32) for i in range(3)]
            st["sgi"] = 0
        steps.append(([], begin))

        def gu_step(j0, jl0):
            def fn(w, bw):
                wv = w[:].rearrange("p (a b) -> p a b", a=8)
                for jj in range(2):
                    jl = jl0 + jj
                    for tt in range(4):
                        ts = slice(tt * 512, (tt + 1) * 512)
                        pg, bpg = self.psum()
                        pu, bpu = self.psum()
                        k.group("pe", [lambda e, kc=kc, pg=pg, jj=jj, ts=ts: e.matmul(
                            pg[:], lhsT=wv[:, kc, jj * 128:(jj + 1) * 128], rhs=self.xnT[:, kc, ts],
                            start=(kc == 0), stop=(kc == 7)) for kc in range(8)],
                            reads=[bw, self.b_xn[tt]], writes=[bpg])
                        k.group("pe", [lambda e, kc=kc, pu=pu, jj=jj, ts=ts: e.matmul(
                            pu[:], lhsT=wv[:, kc, 256 + jj * 128:256 + (jj + 1) * 128], rhs=self.xnT[:, kc, ts],
                            start=(kc == 0), stop=(kc == 7)) for kc in range(8)],
                            reads=[bw, self.b_xn[tt]], writes=[bpu])
                        sg, bsg = st["sg"][st["sgi"] % 3]
                        st["sgi"] += 1
                        k.op("act", lambda e, sg=sg, pg=pg: e.activation(out=sg[:], in_=pg[:], func=AF.Silu),
                             reads=[bpg], writes=[bsg])
                        k.op("dve", lambda e, sg=sg, pu=pu, jl=jl, ts=ts: e.tensor_tensor(
                            out=st["act"][:, jl, ts], in0=sg[:], in1=pu[:], op=ALU.mult),
                            reads=[bsg, bpu], writes=[st["b_act"][jl][tt]])
            loads = [
                (lambda w: w[:].rearrange("p (a b) -> p a b", a=8)[:, :, 0:256],
                 self.w_fg[l][:, j0 * 128:j0 * 128 + 256].rearrange("(kc kp) n -> kp kc n", kp=128)),
                (lambda w: w[:].rearrange("p (a b) -> p a b", a=8)[:, :, 256:512],
                 self.w_fu[l][:, j0 * 128:j0 * 128 + 256].rearrange("(kc kp) n -> kp kc n", kp=128)),
            ]
            return (loads, fn)

        def down_step(ja, jb, half):
            J = jb - ja
            def fn(w, bw):
                wv = w[:].rearrange("p (a b) -> p a b", a=8)
                for c4 in range(4):
                    c2 = half * 4 + c4
                    for tt in range(4):
                        ts = slice(tt * 512, (tt + 1) * 512)
                        ps, bps = self.psum()
                        k.group("pe", [lambda e, jl=jl, ps=ps, c4=c4, ts=ts: e.matmul(
                            ps[:], lhsT=wv[:, jl, c4 * 128:(c4 + 1) * 128], rhs=st["act"][:, jl, ts],
                            start=(jl == 0), stop=(jl == J - 1)) for jl in range(J)],
                            reads=[bw] + [st["b_act"][jl][tt] for jl in range(J)], writes=[bps])
                        k.op("dve", lambda e, ps=ps, c2=c2, ts=ts: e.tensor_tensor(
                            out=self.hT[:, c2, ts], in0=self.hT[:, c2, ts], in1=ps[:], op=ALU.add),
                            reads=[bps], writes=[self.b_hT[c2][tt]])
            loads = [(lambda w: w[:].rearrange("p (a b) -> p a b", a=8)[:, 0:J, :],
                      self.w_fd[l][ja * 128:jb * 128, half * 512:(half + 1) * 512].rearrange("(j kp) n -> kp j n", kp=128))]
            return (loads, fn)

        for (ja, jb) in groups:
            for j0 in range(ja, jb, 2):
                steps.append(gu_step(j0, j0 - ja))
            for half in range(2):
                steps.append(down_step(ja, jb, half))

        def end(w, bw):
            self.arena_close()
        steps.append(([], end))

    def ph_final(self):
        k = self.k
        self.arena_open()
        sq = [self.at(f"fsq{i}", [128, 512], BF16) for i in range(3)]
        lnb = self.at("flnb", [128, 512], F32)
        rstd = self.at("frstd", [128, 512], F32)
        yt = [self.at(f"fyt{i}", [128, 512], F32) for i in range(2)]
        stg = [self.at(f"fstg{i}", [128, 4, D_MODEL], F32) for i in range(2)]
        ones = self.cbf[:, C_ONES:C_ONES + 128]
        ident = self.cst[:, C_IDENT:C_IDENT + 128]
        for tt in range(4):
            ts = slice(tt * 512, (tt + 1) * 512)
            ps, bps = self.psum()
            for c in range(8):
                s, bs = sq[c % 3]
                k.op("act", lambda e, s=s, c=c, ts=ts: e.activation(out=s[:], in_=self.hT[:, c, ts], func=AF.Square),
                     reads=[self.b_hT[c][tt]], writes=[bs])
                k.op("pe", lambda e, ps=ps, s=s, c=c: e.matmul(ps[:], lhsT=ones, rhs=s[:], start=(c == 0), stop=(c == 7)),
                     reads=[bs, self.b_cbf], writes=[bps])
            r, br = rstd
            k.op("act", lambda e, ps=ps: e.activation(out=lnb[0][:], in_=ps[:], func=AF.Ln, scale=1.0 / D_MODEL, bias=EPS),
                 reads=[bps], writes=[lnb[1]])
            k.op("act", lambda e, r=r: e.activation(out=r[:], in_=lnb[0][:], func=AF.Exp, scale=-0.5),
                 reads=[lnb[1]], writes=[br])
            sg, bsg = stg[tt % 2]
            for c in range(8):
                y, by = yt[c % 2]
                k.op("dve", lambda e, c=c, ts=ts, r=r, y=y: e.scalar_tensor_tensor(
                    out=y[:], in0=self.hT[:, c, ts], scalar=self.nrm[:, 2, c:c + 1], in1=r[:],
                    op0=ALU.mult, op1=ALU.mult), reads=[self.b_hT[c][tt], br, self.b_nrm], writes=[by])
                pt, bpt = self.psum()
                k.group("pe", [lambda e, pt=pt, y=y, j=j: e.transpose(
                    out=pt[:, j * 128:(j + 1) * 128], in_=y[:, j * 128:(j + 1) * 128], identity=ident) for j in range(4)],
                    reads=[by, self.b_cst], writes=[bpt])
                k.op("act", lambda e, pt=pt, sg=sg, c=c: e.activation(
                    out=sg[:, :, c * 128:(c + 1) * 128], in_=pt[:].rearrange("p (a b) -> p a b", a=4), func=AF.Copy),
                    reads=[bpt], writes=[bsg])
            bo = Buf(f"out{tt}")
            self.out_bufs.append(bo)
            self.dma("sp", self.out[tt * 512:(tt + 1) * 512, :].rearrange("(tb p) d -> p tb d", p=128), sg[:],
                     reads=[bsg], writes=[bo], sembuf=bsg)
        self.arena_close()


    def build(self):
        self.ph_load()
        steps = []
        for l in range(self.L):
            if self.flags.get("mixer", True):
                steps.append(([], lambda w, bw, l=l: self.ph_norm(0, l)))
                self.ph_mixers(l, steps)
            if self.flags.get("ffn", True):
                steps.append(([], lambda w, bw, l=l: self.ph_norm(1, l)))
                self.ph_ffn(l, steps)
        self.run_steps(steps)
        self.ph_final()
        self.k.finish(self.out_bufs + list(self.dbg_bufs))
        self.k.emit()


def build_program(n_layers=DEPTH, dbg=None, flags=None):
    nc = bass.Bass("TRN2", target_bir_lowering=False)
    with ExitStack() as st:
        p = Prog(nc, st, n_layers, dbg=dbg, flags=flags)
        p.build()
        stats = (p.k.n_inst, p.k.n_wait)
    return nc, stats


def fm_vec(v):
    return np.ascontiguousarray(np.asarray(v, np.float32).reshape(8, 128).T)


def make_cpar(inputs, L):
    out = np.zeros((L, 128, 4, 8), np.float32)
    for l in range(L):
        cols = [inputs["c_conv_w"][l][j] for j in range(4)] + [inputs["c_conv_b"][l], inputs["c_b_a"][l],
                                                               inputs["c_b_x"][l], inputs["c_lambda"][l]]
        for i, v in enumerate(cols):
            out[l, :, :, i] = np.asarray(v, np.float32).reshape(4, 128).T
    return out


def make_biasT(inputs, L):
    kk = np.arange(128)[:, None, None]
    jj = np.arange(5)[None, :, None]
    qq = np.arange(128)[None, None, :]
    idx = np.clip(qq - kk + (4 - jj) * 128, -128, 128) + 128
    tab = np.asarray(inputs["b_rel_bias"], np.float32)[:L]
    g = tab[:, :, idx]
    return np.ascontiguousarray(g.transpose(0, 2, 1, 3, 4))


def make_maskb():
    kk = np.arange(128)[:, None, None]
    jj = np.arange(5)[None, :, None]
    qq = np.arange(128)[None, None, :]
    diff = 8 - 2 * jj + (qq >= 64) - (kk >= 64)
    return np.where((diff >= 0) & (diff <= 8), 0.0, -30000.0).astype(np.float32)


def make_in_maps(inputs, n_layers=DEPTH):
    L = n_layers
    f = lambda a: np.ascontiguousarray(np.asarray(a, np.float32))
    shared = {
        "consts": make_consts(),
        "norm_mix": np.stack([fm_vec(inputs["norm_mix"][l]) for l in range(L)]),
        "norm_ffn": np.stack([fm_vec(inputs["norm_ffn"][l]) for l in range(L)]),
        "norm_final": fm_vec(inputs["norm_final"]),
        "w_in": f(inputs["w_in"][:L]),
        "w_branch": f(inputs["w_branch"][:L]),
        "cpar": make_cpar(inputs, L),
        "c_w_a": f(inputs["c_w_a"][:L]),
        "c_w_x": f(inputs["c_w_x"][:L]),
        "b_biasT": make_biasT(inputs, L),
        "maskb": make_maskb(),
        "a_w_gk": f(inputs["a_w_gk"][:L]),
        "a_bgk": np.ascontiguousarray(np.broadcast_to(np.asarray(inputs["a_b_gk"], np.float32)[:L, None, :], (L, 128, 256))),
        "a_ng": np.ascontiguousarray(np.broadcast_to(np.asarray(inputs["a_norm"], np.float32)[:L, None, :], (L, 128, 128))),
        "w_out": f(inputs["w_out"][:L]),
        "w_ffn_gate": f(inputs["w_ffn_gate"][:L]),
        "w_ffn_up": f(inputs["w_ffn_up"][:L]),
        "w_ffn_down": f(inputs["w_ffn_down"][:L]),
    }
    x = f(inputs["x"])
    return [dict(shared, x=x[b]) for b in range(x.shape[0])]


_CACHE = {}


def kernel(**inputs):
    if "nc" not in _CACHE:
        _CACHE["nc"] = build_program(DEPTH)[0]
    nc = _CACHE["nc"]
    in_maps = make_in_maps(inputs)
    res = run_bass_kernel_spmd(nc, in_maps, core_ids=list(range(NCORES)))
    return np.stack([np.asarray(r["out"], np.float32) for r in res.results], axis=0)


def _ph_mixers(self, l, steps):
    en = self.flags.get("branches", (1, 1, 1, 1))
    fns = [self.mix_a, self.mix_b, self.mix_c, self.mix_d]

    def zero_step(slot):
        def zero(w, bw, slot=slot):
            for c in range(4):
                self.k.op("pool", lambda e, c=c: e.memset(self.yT[:, slot * 4 + c, :], 0.0),
                          writes=[self.b_yT[slot * 4 + c][t] for t in range(4)])
        return ([], zero)

    for slot, n in enumerate((0, 3)):
        if en[n]:
            fns[n](l, steps, slot)
        else:
            steps.append(zero_step(slot))
    self.ph_merge(l, steps, (0, 3))
    if en[1] and en[2] and self.flags.get("cosched", True):
        sb, sc = [], []
        self.mix_b(l, sb, 0, ext=True)
        self.mix_c(l, sc, 1, ext=True)
        steps.append(([], lambda w, bw: self.arena_open()))
        steps.append(sb[0])
        steps.append(sc[0])
        B_, C_ = ((l, 0),), ((l, 1),)
        order = [sb[1] + B_, sc[1] + C_, sb[2] + B_, sb[3] + B_, sc[2] + C_, sb[4] + B_]

        def first(w, bw, f=order[0][1]):
            self.k.sched_begin()
            f(w, bw)

        def last(w, bw, f=order[-1][1]):
            f(w, bw)
            self.k.sched_end(("bc", 0))
        order[0] = (order[0][0], first, order[0][2])
        order[-1] = (order[-1][0], last, order[-1][2])
        steps.extend(order)
        steps.append(([], lambda w, bw: self.arena_close()))
    else:
        for slot, n in enumerate((1, 2)):
            if en[n]:
                fns[n](l, steps, slot)
            else:
                steps.append(zero_step(slot))
    self.ph_merge(l, steps, (1, 2))


def _ph_merge(self, l, steps, pair):
    k = self.k
    st = {}

    def begin(w, bw):
        self.arena_open()
        st["T1"] = self.sb("mg_T1", [128, 2, SEQ], F32, st=self.ar)
        st["b_T1"] = [[self.abuf(f"T1_{c}_{t}") for t in range(4)] for c in range(2)]
        st["M"] = self.sb("mg_M", [128, 4, SEQ], BF16, st=self.ar)
        st["b_M"] = [[self.abuf(f"M_{c}_{t}") for t in range(4)] for c in range(4)]
        st["s"] = [self.at(f"mg_s{i}", [128, 512], F32) for i in range(3)]
        st["t2"] = [self.at(f"mg_t{i}", [128, 512], F32) for i in range(2)]
        st["i"] = 0
    steps.append(([], begin))

    def gw_step(slot, cp):
        n = pair[slot]
        c0 = cp * 2

        def fn(w, bw):
            gv = w[:, 0:2048].rearrange("p (a b) -> p a b", a=8)
            bv = w[:, 2048:3072].rearrange("p (a b) -> p a b", a=4)
            for cc in range(2):
                c = c0 + cc
                for tt in range(4):
                    ts = slice(tt * 512, (tt + 1) * 512)
                    pg, bpg = self.psum()
                    pw, bpw = self.psum()
                    k.group("pe", [lambda e, kc=kc, pg=pg, cc=cc, ts=ts: e.matmul(
                        pg[:], lhsT=gv[:, kc, cc * 128:(cc + 1) * 128], rhs=self.xnT[:, kc, ts],
                        start=(kc == 0), stop=(kc == 7)) for kc in range(8)],
                        reads=[bw, self.b_xn[tt]], writes=[bpg])
                    k.group("pe", [lambda e, kc=kc, pw=pw, cc=cc, ts=ts: e.matmul(
                        pw[:], lhsT=bv[:, kc, cc * 128:(cc + 1) * 128], rhs=self.yT[:, slot * 4 + kc, ts],
                        start=(kc == 0), stop=(kc == 3)) for kc in range(4)],
                        reads=[bw] + [self.b_yT[slot * 4 + kc][tt] for kc in range(4)], writes=[bpw])
                    s, bs = st["s"][st["i"] % 3]
                    st["i"] += 1
                    k.op("act", lambda e, s=s, pg=pg: e.activation(out=s[:], in_=pg[:], func=AF.Sigmoid),
                         reads=[bpg], writes=[bs])
                    if slot == 0:
                        k.op("dve", lambda e, s=s, pw=pw, cc=cc, ts=ts: e.tensor_tensor(
                            out=st["T1"][:, cc, ts], in0=s[:], in1=pw[:], op=ALU.mult),
                            reads=[bs, bpw], writes=[st["b_T1"][cc][tt]])
                    else:
                        t2, bt2 = st["t2"][st["i"] % 2]
                        k.op("dve", lambda e, s=s, pw=pw, t2=t2: e.tensor_tensor(
                            out=t2[:], in0=s[:], in1=pw[:], op=ALU.mult), reads=[bs, bpw], writes=[bt2])
                        cm = c % 4
                        k.op("pool", lambda e, t2=t2, cc=cc, cm=cm, ts=ts: e.tensor_tensor(
                            out=st["M"][:, cm, ts], in0=st["T1"][:, cc, ts], in1=t2[:], op=ALU.add),
                            reads=[bt2, st["b_T1"][cc][tt]], writes=[st["b_M"][cm][tt]])
        loads = [
            (lambda w: w[:, 0:2048].rearrange("p (a b) -> p a b", a=8),
             self.w_in[l][:, O_GATE + n * 1024 + c0 * 128:O_GATE + n * 1024 + c0 * 128 + 256].rearrange(
                 "(kc kp) n -> kp kc n", kp=128)),
            (lambda w: w[:, 2048:3072].rearrange("p (a b) -> p a b", a=4),
             self.w_branch[l][n][:, c0 * 128:c0 * 128 + 256].rearrange("(kc kp) n -> kp kc n", kp=128)),
        ]
        return (loads, fn)

    def out_step(cg):
        def fn(w, bw):
            wv = w[:].rearrange("p (a b) -> p a b", a=4)
            for c2 in range(8):
                for tt in range(4):
                    ts = slice(tt * 512, (tt + 1) * 512)
                    ps, bps = self.psum()
                    k.group("pe", [lambda e, j=j, ps=ps, c2=c2, ts=ts: e.matmul(
                        ps[:], lhsT=wv[:, j, c2 * 128:(c2 + 1) * 128], rhs=st["M"][:, j, ts],
                        start=(j == 0), stop=(j == 3)) for j in range(4)],
                        reads=[bw] + [st["b_M"][j][tt] for j in range(4)], writes=[bps])
                    k.op("dve", lambda e, ps=ps, c2=c2, ts=ts: e.tensor_tensor(
                        out=self.hT[:, c2, ts], in0=self.hT[:, c2, ts], in1=ps[:], op=ALU.add),
                        reads=[bps], writes=[self.b_hT[c2][tt]])
        loads = [(lambda w: w[:].rearrange("p (a b) -> p a b", a=4),
                  self.w_out[l][cg * 512:(cg + 1) * 512, :].rearrange("(kc kp) n -> kp kc n", kp=128))]
        return (loads, fn)

    for cg in range(2):
        for cpl in range(2):
            cp = cg * 2 + cpl
            steps.append(gw_step(0, cp))
            steps.append(gw_step(1, cp))
        steps.append(out_step(cg))

    def end(w, bw):
        self.arena_close()
    steps.append(([], end))


Prog.ph_mixers = _ph_mixers
Prog.ph_merge = _ph_merge


def _mix_c(self, l, steps, slot, ext=False):
    k = self.k
    st = {}
    PSC = range(5, 8) if ext else range(8)
    N = 512

    NSET = 1 if ext else 2
    NXIN = 1 if ext else 2

    def begin(w, bw):
        if not ext:
            self.arena_open()
        st["cp"] = self.at("c_par", [128, 4, 8], F32)
        st["cl"] = self.at("c_cl", [128, 4, 2], F32)
        st["wbd"] = self.at("c_wbd", [128, 8, 128], BF16)
        st["xin"] = [self.at(f"c_xin{i}", [128, 3 + SEQ], F32) for i in range(NXIN)]
        names = ("gg", "xc", "ra", "ibx", "a2", "hs")
        st["tmp"] = [{nm: self.at(f"c_{nm}{i}", [128, N], F32) for nm in names} for i in range(NSET)]
        for i in range(NSET):
            st["tmp"][i]["xcb"] = self.at(f"c_xcb{i}", [128, N], BF16)
        st["ui"] = 0
        cp, bcp = st["cp"]
        cl, bcl = st["cl"]
        wbd, bwbd = st["wbd"]
        self.dma("sp", cp[:], self.cpar[l], writes=[bcp])
        hs_t, bwst = st["tmp"][0]["hs"]
        wst = hs_t[:, 0:512].rearrange("p (a b) -> p a b", a=8)
        k.op("dve", lambda e: e.memset(wbd[:], 0.0), writes=[bwbd])
        for which, srcw in enumerate((self.c_w_a, self.c_w_x)):
            for g in range(2):
                self.dma("sp", wst[g * 64:(g + 1) * 64, which * 4:(which + 1) * 4, :],
                         srcw[l].rearrange("(cc g) r c -> g r cc c", g=2)[g], writes=[bwst])
        for g in range(2):
            k.op("dve", lambda e, g=g: e.tensor_copy(out=wbd[g * 64:(g + 1) * 64, :, g * 64:(g + 1) * 64],
                                                     in_=wst[g * 64:(g + 1) * 64, :, :]),
                 reads=[bwst], writes=[bwbd])
        k.op("act", lambda e: e.activation(out=cl[:, :, 0], in_=cp[:, :, 7], func=AF.Exp, scale=-1.0),
             reads=[bcp], writes=[bcl])
        k.op("act", lambda e: e.activation(out=cl[:, :, 0], in_=cl[:, :, 0], func=AF.Ln, bias=1.0),
             reads=[bcl], writes=[bcl])
        k.op("dve", lambda e: e.tensor_scalar(out=cl[:, :, 1], in0=cl[:, :, 0], scalar1=-16.0, scalar2=None, op0=ALU.mult),
             reads=[bcl], writes=[bcl])
        k.op("dve", lambda e: e.tensor_scalar(out=cl[:, :, 0], in0=cl[:, :, 0], scalar1=-8.0, scalar2=None, op0=ALU.mult),
             reads=[bcl], writes=[bcl])
        for i in range(NXIN):
            xin, bxin = st["xin"][i]
            k.op("dve", lambda e, xin=xin: e.memset(xin[:, 0:3], 0.0), writes=[bxin])
    steps.append(([], begin))

    def chunk_step(cpair):
        def fn(w, bw):
            if ext:
                return fn_(w, bw)
            k.sched_begin()
            fn_(w, bw)
            k.sched_end(("c", cpair))

        def fn_(w, bw):
            wv = w[:].rearrange("p (a b) -> p a b", a=8)
            cp, bcp = st["cp"]
            cl, bcl = st["cl"]
            wbd, bwbd = st["wbd"]
            for j in range(2):
                cc = cpair * 2 + j
                xin, bxin = st["xin"][cc % NXIN]
                for tt in range(4):
                    ts = slice(tt * 512, (tt + 1) * 512)
                    px, bpx = self.psum(PSC)
                    k.group("pe", [lambda e, kc=kc, px=px, j=j, ts=ts: e.matmul(
                        px[:], lhsT=wv[:, kc, 256 + j * 128:256 + (j + 1) * 128], rhs=self.xnT[:, kc, ts],
                        start=(kc == 0), stop=(kc == 7)) for kc in range(8)],
                        reads=[bw, self.b_xn[tt]], writes=[bpx])
                    k.op("dve", lambda e, px=px, xin=xin, tt=tt: e.tensor_copy(
                        out=xin[:, 3 + tt * 512:3 + (tt + 1) * 512], in_=px[:]),
                        reads=[bpx], writes=[bxin])
                prev_hs = None
                for tt in range(4):
                    ts = slice(tt * 512, (tt + 1) * 512)
                    T = st["tmp"][st["ui"] % NSET]
                    st["ui"] += 1
                    gg, bgg = T["gg"]; xc, bxc = T["xc"]; xcb, bxcb = T["xcb"]; ra, bra = T["ra"]
                    ibx, bibx = T["ibx"]; a2, ba2 = T["a2"]; hs, bhs = T["hs"]
                    pgt, bpgt = self.psum(PSC)
                    k.group("pe", [lambda e, kc=kc, pgt=pgt, j=j, ts=ts: e.matmul(
                        pgt[:], lhsT=wv[:, kc, j * 128:(j + 1) * 128], rhs=self.xnT[:, kc, ts],
                        start=(kc == 0), stop=(kc == 7)) for kc in range(8)],
                        reads=[bw, self.b_xn[tt]], writes=[bpgt])
                    k.op("act", lambda e, gg=gg, pgt=pgt: e.activation(out=gg[:], in_=pgt[:], func=AF.Gelu_apprx_tanh),
                         reads=[bpgt], writes=[bgg], tab="gelu")
                    o = tt * 512
                    k.op("dve", lambda e, xc=xc, xin=xin, o=o, cc=cc: e.tensor_scalar(
                        out=xc[:], in0=xin[:, o + 3:o + 3 + N], scalar1=cp[:, cc, 3:4], scalar2=cp[:, cc, 4:5],
                        op0=ALU.mult, op1=ALU.add), reads=[bxin, bcp], writes=[bxc])
                    for jj in range(3):
                        k.op("dve", lambda e, xc=xc, xin=xin, o=o, cc=cc, jj=jj: e.scalar_tensor_tensor(
                            out=xc[:], in0=xin[:, o + jj:o + jj + N], scalar=cp[:, cc, jj:jj + 1], in1=xc[:],
                            op0=ALU.mult, op1=ALU.add), reads=[bxin, bcp, bxc], writes=[bxc])
                    k.op("dve", lambda e, xc=xc, xcb=xcb: e.tensor_copy(out=xcb[:], in_=xc[:]), reads=[bxc], writes=[bxcb])
                    pr, bpr = self.psum(PSC)
                    k.op("pe", lambda e, pr=pr, xcb=xcb, cc=cc: e.matmul(pr[:], lhsT=wbd[:, cc, :], rhs=xcb[:], start=True, stop=True),
                         reads=[bwbd, bxcb], writes=[bpr])
                    pi, bpi = self.psum(PSC)
                    k.op("pe", lambda e, pi=pi, xcb=xcb, cc=cc: e.matmul(pi[:], lhsT=wbd[:, 4 + cc, :], rhs=xcb[:], start=True, stop=True),
                         reads=[bwbd, bxcb], writes=[bpi])
                    k.op("act", lambda e, ra=ra, pr=pr, cc=cc: e.activation(out=ra[:], in_=pr[:], func=AF.Sigmoid, bias=cp[:, cc, 5:6]),
                         reads=[bpr, bcp], writes=[bra], tab="sig")
                    k.op("act", lambda e, ibx=ibx, pi=pi, cc=cc: e.activation(out=ibx[:], in_=pi[:], func=AF.Sigmoid, bias=cp[:, cc, 6:7]),
                         reads=[bpi, bcp], writes=[bibx], tab="sig")
                    k.op("act", lambda e, ra=ra, cc=cc: e.activation(out=ra[:], in_=ra[:], func=AF.Exp, scale=cl[:, cc, 0:1]),
                         reads=[bra, bcl], writes=[bra], tab="expln")
                    k.op("pool", lambda e, a2=a2, ra=ra: e.tensor_tensor(out=a2[:], in0=ra[:], in1=ra[:], op=ALU.mult),
                         reads=[bra], writes=[ba2])
                    k.op("act", lambda e, a2=a2: e.activation(out=a2[:], in_=a2[:], func=AF.Ln, scale=-1.0, bias=1.0),
                         reads=[ba2], writes=[ba2], tab="expln")
                    k.op("act", lambda e, a2=a2: e.activation(out=a2[:], in_=a2[:], func=AF.Exp, scale=0.5),
                         reads=[ba2], writes=[ba2], tab="expln")
                    k.op("dve", lambda e, ibx=ibx, xc=xc: e.tensor_tensor(out=ibx[:], in0=ibx[:], in1=xc[:], op=ALU.mult),
                         reads=[bibx, bxc], writes=[bibx])
                    k.op("pool", lambda e, ibx=ibx, a2=a2: e.tensor_tensor(out=ibx[:], in0=ibx[:], in1=a2[:], op=ALU.mult),
                         reads=[bibx, ba2], writes=[bibx])
                    init = 0.0 if prev_hs is None else prev_hs[0][:, N - 1:N]
                    rd = [bra, bibx] + ([prev_hs[1]] if prev_hs is not None else [])
                    k.op("dve", lambda e, hs=hs, ra=ra, ibx=ibx, init=init: e.tensor_tensor_scan(
                        out=hs[:], data0=ra[:], data1=ibx[:], initial=init, op0=ALU.mult, op1=ALU.add),
                        reads=rd, writes=[bhs])
                    prev_hs = (hs, bhs)
                    k.op("pool", lambda e, hs=hs, gg=gg, cc=cc, ts=ts: e.tensor_tensor(
                        out=self.yT[:, slot * 4 + cc, ts], in0=hs[:], in1=gg[:], op=ALU.mult),
                        reads=[bhs, bgg], writes=[self.b_yT[slot * 4 + cc][tt]])
        loads = [
            (lambda w: w[:].rearrange("p (a b) -> p a b", a=8)[:, :, 0:256],
             self.w_in[l][:, O_CG + cpair * 256:O_CG + cpair * 256 + 256].rearrange("(kc kp) n -> kp kc n", kp=128)),
            (lambda w: w[:].rearrange("p (a b) -> p a b", a=8)[:, :, 256:512],
             self.w_in[l][:, O_CX + cpair * 256:O_CX + cpair * 256 + 256].rearrange("(kc kp) n -> kp kc n", kp=128)),
        ]
        return (loads, fn)

    for cpair in range(2):
        steps.append(chunk_step(cpair))

    def end(w, bw):
        self.arena_close()
    if not ext:
        steps.append(([], end))


Prog.mix_c = _mix_c
Prog.mix_a = _mix_c
Prog.mix_b = _mix_c
Prog.mix_d = _mix_c


def pipeline(n, stages):
    ns = len(stages)
    for t in range(n + ns - 1):
        for s in range(ns):
            i = t - s
            if 0 <= i < n:
                stages[s](i)


def _mix_b(self, l, steps, slot, ext=False):
    k = self.k
    st = {}
    PSB = range(5) if ext else range(8)
    assert len(PSB) >= 5

    def begin(w, bw):
        if not ext:
            self.arena_open()
        st["qT"] = self.at("b_qT", [128, SEQ], BF16)
        st["kTp"] = [self.at(f"b_kTp{h}", [128, SEQ], BF16) for h in range(2)]
        st["vt"] = self.at("b_vt", [128, 16, 2, 65], BF16)
        st["bias"] = self.at("b_bias", [128, 5, 128], F32)
        st["mask"] = self.at("b_mask", [128, 5, 128], BF16)
        st["biasb"] = self.at("b_biasb", [128, 2, 5, 128], BF16)
        st["pT"] = [self.at(f"b_pT{i}", [128, 5, 128], BF16) for i in range(3)]
        st["ri"] = [self.at(f"b_ri{i}", [128, 2], F32) for i in range(2)]
        st["ytm"] = [self.at(f"b_ytm{i}", [128, 4, 128], BF16) for i in range(1 if ext else 2)]
        self.dma("pool", st["mask"][0][:], self.maskb[:, :, :], writes=[st["mask"][1]])
        vt, bvt = st["vt"]
        k.op("dve", lambda e: e.memset(vt[:, :, :, 64:65], 1.0), writes=[bvt])
        for h in range(2):
            t, tb_ = st["kTp"][h]
            o = (1 - h) * 64
            k.op("dve", lambda e, t=t, o=o: e.memset(t[o:o + 64, :], 0.0), writes=[tb_])
    steps.append(([], begin))

    def pair_step(p):
        def fn(w, bw):
            if ext:
                return fn_(w, bw)
            k.sched_begin()
            fn_(w, bw)
            k.sched_end(("b", p))

        def fn_(w, bw):
            wv = w[:].rearrange("p (a b) -> p a b", a=8)
            qT, bqT = st["qT"]; kTp = st["kTp"]; vt, bvt = st["vt"]
            bias, bbias = st["bias"]; mask, bmask = st["mask"]; biasb, bbiasb = st["biasb"]
            ident = self.cbf[:, C_IDENT:C_IDENT + 128]
            for hh in range(2):
                self.dma("sp", bias[:], self.b_biasT[l][:, 2 * p + hh, :, :], writes=[bbias])
                k.op("pool", lambda e, hh=hh: e.tensor_tensor(out=biasb[:, hh], in0=bias[:], in1=mask[:], op=ALU.add),
                     reads=[bbias, bmask], writes=[bbiasb], n=640)
            for tt in range(4):
                ts = slice(tt * 512, (tt + 1) * 512)
                for which in range(2):
                    ps, bps = self.psum(PSB)
                    k.group("pe", [lambda e, kc=kc, ps=ps, which=which, ts=ts: e.matmul(
                        ps[:], lhsT=wv[:, kc, which * 128:(which + 1) * 128], rhs=self.xnT[:, kc, ts],
                        start=(kc == 0), stop=(kc == 7)) for kc in range(8)],
                        reads=[bw, self.b_xn[tt]], writes=[bps])
                    if which == 0:
                        k.op("act", lambda e, ps=ps, ts=ts: e.activation(out=qT[:, ts], in_=ps[:], func=AF.Identity, scale=0.125),
                             reads=[bps], writes=[bqT])
                    else:
                        for h in range(2):
                            t, tb_ = kTp[h]
                            k.op("dve", lambda e, ps=ps, ts=ts, t=t, h=h: e.tensor_copy(
                                out=t[h * 64:(h + 1) * 64, ts], in_=ps[h * 64:(h + 1) * 64, :]), reads=[bps], writes=[tb_])
                ps, bps = self.psum(PSB)
                fns = []
                for j in range(4):
                    tb = tt * 4 + j
                    for kc in range(8):
                        fns.append(lambda e, kc=kc, ps=ps, j=j, tb=tb: e.matmul(
                            ps[:, j * 128:(j + 1) * 128], lhsT=self.xnT[:, kc, tb * 128:(tb + 1) * 128], rhs=wv[:, kc, 256:384],
                            start=(kc == 0), stop=(kc == 7)))
                k.group("pe", fns, reads=[bw, self.b_xn[tt]], writes=[bps])
                k.op("act", lambda e, ps=ps, tt=tt: e.activation(
                    out=vt[:, tt * 4:(tt + 1) * 4, :, 0:64], in_=ps[:].rearrange("p (a h d) -> p a h d", a=4, h=2),
                    func=AF.Copy), reads=[bps], writes=[bvt])
            po = None
            for qb in range(16):
                qs = slice(qb * 128, (qb + 1) * 128)
                j0 = max(0, 4 - qb)
                if qb % 4 == 0:
                    ytm, bytm = st["ytm"][(qb // 4) % len(st["ytm"])]
                po, bpo = self.psum(PSB)
                for hh in range(2):
                    hs = slice(hh * 64, (hh + 1) * 64)
                    kt, bkt = kTp[hh]
                    pT, bpT = st["pT"][(qb * 2 + hh) % 3]
                    ps1, bps1 = self.psum(PSB)
                    ps2, bps2 = self.psum(PSB)
                    identb = self.cbf[:, C_IDENT:C_IDENT + 128]
                    fns = []
                    for j in range(j0, 4):
                        kb = qb - 4 + j
                        fns.append(lambda e, j=j, kb=kb, ps1=ps1, kt=kt, qs=qs: e.matmul(
                            ps1[:, j * 128:(j + 1) * 128], lhsT=kt[:, kb * 128:(kb + 1) * 128], rhs=qT[:, qs],
                            start=True, stop=False, skip_group_check=True))
                        fns.append(lambda e, j=j, ps1=ps1, hh=hh: e.matmul(
                            ps1[:, j * 128:(j + 1) * 128], lhsT=identb, rhs=biasb[:, hh, j, :],
                            start=False, stop=True, skip_group_check=True))
                    if fns:
                        k.group("pe", fns, reads=[bkt, bqT, bbiasb, self.b_cbf], writes=[bps1], n=128)
                    k.group("pe", [
                        lambda e, ps2=ps2, kt=kt, qs=qs: e.matmul(ps2[:, 0:128], lhsT=kt[:, qs], rhs=qT[:, qs], start=True, stop=False),
                        lambda e, ps2=ps2, hh=hh: e.matmul(ps2[:, 0:128], lhsT=identb, rhs=biasb[:, hh, 4, :], start=False, stop=True)],
                        reads=[bkt, bqT, bbiasb, self.b_cbf], writes=[bps2], n=128)
                    if j0 < 4:
                        k.op("act", lambda e, pT=pT, ps1=ps1, j0=j0: e.activation(
                            out=pT[:, j0:4, :], in_=ps1[:].rearrange("p (a b) -> p a b", a=4)[:, j0:4, :], func=AF.Exp),
                            reads=[bps1], writes=[bpT], n=(4 - j0) * 128, tab="expln")
                    k.op("act", lambda e, pT=pT, ps2=ps2: e.activation(out=pT[:, 4, :], in_=ps2[:, 0:128], func=AF.Exp),
                         reads=[bps2], writes=[bpT], n=128, tab="expln")
                    k.group("pe", [lambda e, j=j, pT=pT, po=po, hh=hh, qb=qb, j0=j0: e.matmul(
                        po[:, hh * 65:(hh + 1) * 65], lhsT=pT[:, j, :], rhs=vt[:, qb - 4 + j, hh, :],
                        start=(j == j0), stop=(j == 4)) for j in range(j0, 5)],
                        reads=[bpT, bvt], writes=[bpo], n=65)
                ri, bri = st["ri"][qb % 2]
                pov = po[:, 0:130].rearrange("p (h d) -> p h d", h=2)
                k.op("dve", lambda e, ri=ri, pov=pov: e.reciprocal(out=ri[:], in_=pov[:, :, 64]), reads=[bpo], writes=[bri])
                for hh in range(2):
                    k.op("dve", lambda e, ri=ri, pov=pov, hh=hh, ytm=ytm, qb=qb: e.tensor_scalar(
                        out=ytm[:, qb % 4, hh * 64:(hh + 1) * 64], in0=pov[:, hh, 0:64], scalar1=ri[:, hh:hh + 1], scalar2=None,
                        op0=ALU.mult), reads=[bpo, bri], writes=[bytm])
                if qb % 4 == 3:
                    pt, bpt = self.psum(PSB)
                    ptb = pt[:].bitcast(BF16)
                    k.group("pe", [lambda e, j=j, ytm=ytm, ptb=ptb: e.transpose(
                        out=ptb[:, j * 128:(j + 1) * 128], in_=ytm[:, j, :], identity=ident) for j in range(4)],
                        reads=[bytm, self.b_cbf], writes=[bpt])
                    tt = qb // 4
                    k.op("act", lambda e, ptb=ptb, tt=tt: e.activation(
                        out=self.yT[:, slot * 4 + p, tt * 512:(tt + 1) * 512], in_=ptb[:, 0:512], func=AF.Copy),
                        reads=[bpt], writes=[self.b_yT[slot * 4 + p][tt]])
        loads = [
            (lambda w, i=i: w[:].rearrange("p (a b) -> p a b", a=8)[:, :, i * 128:(i + 1) * 128],
             self.w_in[l][:, off + p * 128:off + (p + 1) * 128].rearrange("(kc kp) n -> kp kc n", kp=128))
            for i, off in enumerate((O_BQ, O_BK, O_BV))]
        return (loads, fn)

    for p in range(4):
        steps.append(pair_step(p))

    def end(w, bw):
        self.arena_close()
    if not ext:
        steps.append(([], end))


Prog.mix_b = _mix_b


def _mix_d(self, l, steps, slot):
    k = self.k
    st = {}

    def begin(w, bw):
        self.arena_open()
        st["qT"] = self.at("d_qT", [128, SEQ], BF16)
        st["nqT"] = self.at("d_nqT", [128, SEQ], BF16)
        st["kTp"] = [self.at(f"d_kTp{h}", [128, SEQ], BF16) for h in range(2)]
        st["vtp"] = self.at("d_vtp", [128, 16, 2, 128], BF16)
        st["e"] = [self.at(f"d_e{i}", [128, 512], F32) for i in range(3)]
        st["lg"] = [self.at(f"d_lg{i}", [128, 512], BF16) for i in range(3)]
        st["tmp"] = [self.at(f"d_tmp{i}", [128, 512], F32) for i in range(3)]
        st["pT"] = [self.at(f"d_pT{i}", [128, 512], BF16) for i in range(3)]
        st["R"] = [self.at(f"d_R{i}", [128, 512], F32) for i in range(2)]
        st["cnt"] = 0
        st["qrc"] = 0
        for h in range(2):
            t, b = st["kTp"][h]
            o = (1 - h) * 64
            k.op("dve", lambda e, t=t, o=o: e.memset(t[o:o + 64, :], 0.0), writes=[b])
        vtp, bvtp = st["vtp"]
        k.op("dve", lambda e: e.memset(vtp[:], 0.0), writes=[bvtp])
    steps.append(([], begin))

    def pair_step(p):
        def fn(w, bw):
            k.sched_begin()
            fn_(w, bw)
            k.sched_end(("d", p))

        def fn_(w, bw):
            wv = w[:].rearrange("p (a b) -> p a b", a=8)
            qT, bqT = st["qT"]; nqT, bnqT = st["nqT"]; vtp, bvtp = st["vtp"]
            kTp = st["kTp"]
            uincl = self.cbf[:, C_UINCL:C_UINCL + 128]
            ones = self.cbf[:, C_ONES:C_ONES + 128]
            m01 = self.cbf[:, C_MASK01:C_MASK01 + 128]
            for tt in range(4):
                ts = slice(tt * 512, (tt + 1) * 512)
                ps, bps = self.psum(range(6))
                k.group("pe", [lambda e, kc=kc, ps=ps, ts=ts: e.matmul(
                    ps[:], lhsT=wv[:, kc, 0:128], rhs=self.xnT[:, kc, ts], start=(kc == 0), stop=(kc == 7)) for kc in range(8)],
                    reads=[bw, self.b_xn[tt]], writes=[bps])
                k.op("act", lambda e, ps=ps, ts=ts: e.activation(out=qT[:, ts], in_=ps[:], func=AF.Copy), reads=[bps], writes=[bqT])
                k.op("act", lambda e, ts=ts: e.mul(nqT[:, ts], qT[:, ts], -0.125), reads=[bqT], writes=[bnqT])
                ps, bps = self.psum(range(6))
                k.group("pe", [lambda e, kc=kc, ps=ps, ts=ts: e.matmul(
                    ps[:], lhsT=wv[:, kc, 128:256], rhs=self.xnT[:, kc, ts], start=(kc == 0), stop=(kc == 7)) for kc in range(8)],
                    reads=[bw, self.b_xn[tt]], writes=[bps])
                for h in range(2):
                    t, b = kTp[h]
                    k.op("dve", lambda e, ps=ps, ts=ts, t=t, h=h: e.tensor_copy(
                        out=t[h * 64:(h + 1) * 64, ts], in_=ps[h * 64:(h + 1) * 64, :]), reads=[bps], writes=[b])
                ps, bps = self.psum(range(6))
                fns = []
                for j in range(4):
                    tb = tt * 4 + j
                    for kc in range(8):
                        fns.append(lambda e, kc=kc, ps=ps, j=j, tb=tb: e.matmul(
                            ps[:, j * 128:(j + 1) * 128], lhsT=self.xnT[:, kc, tb * 128:(tb + 1) * 128], rhs=wv[:, kc, 256:384],
                            start=(kc == 0), stop=(kc == 7)))
                k.group("pe", fns, reads=[bw, self.b_xn[tt]], writes=[bps])
                for h in range(2):
                    k.op("dve", lambda e, ps=ps, tt=tt, h=h: e.tensor_copy(
                        out=vtp[:, tt * 4:(tt + 1) * 4, h, h * 64:(h + 1) * 64],
                        in_=ps[:].rearrange("p (a d) -> p a d", a=4)[:, :, h * 64:(h + 1) * 64]), reads=[bps], writes=[bvtp], n=256)

            for qr in range(self.flags.get("d_qr", 4)):
                qi = st["qrc"]; st["qrc"] += 1
                po, bpo = self.ps[6 + qi % 2], self.b_ps[6 + qi % 2]
                Rs = st["R"]
                for h in range(2):
                    k.op("dve", lambda e, h=h: e.memset(Rs[h][0][:], 0.0), writes=[Rs[h][1]])
                nkb = qr * 4 + 4
                first = [True]
                for kb in range(nkb - 1, -1, -1):
                    c0 = max(0, (kb - qr * 4) * 128)
                    diag = kb >= qr * 4
                    kbs = slice(kb * 128, (kb + 1) * 128)
                    qsl = slice(qr * 512 + c0, (qr + 1) * 512)
                    wn = 512 - c0
                    for h in range(2):
                        ci = st["cnt"]; st["cnt"] += 1
                        kt, bkt = kTp[h]
                        R, bR = Rs[h]
                        e_, be = st["e"][ci % 3]; lg, blg = st["lg"][ci % 3]
                        tmp, btmp = st["tmp"][ci % 3]; pT, bpT = st["pT"][ci % 3]
                        pss, bpss = self.psum(range(6))
                        k.op("pe", lambda e, pss=pss, kt=kt, kbs=kbs, qsl=qsl, c0=c0: e.matmul(
                            pss[:, c0:512], lhsT=kt[:, kbs], rhs=qT[:, qsl], start=True, stop=True),
                            reads=[bkt, bqT], writes=[bpss], n=wn)
                        k.op("act", lambda e, pss=pss, e_=e_, c0=c0: e.activation(
                            out=e_[:, c0:512], in_=pss[:, c0:512], func=AF.Exp, scale=0.125), reads=[bpss], writes=[be], n=wn)
                        k.op("act", lambda e, lg=lg, e_=e_, c0=c0: e.activation(
                            out=lg[:, c0:512], in_=e_[:, c0:512], func=AF.Ln, bias=1.0), reads=[be], writes=[blg], n=wn)
                        if diag:
                            k.op("dve", lambda e, lg=lg, c0=c0: e.tensor_tensor(
                                out=lg[:, c0:c0 + 128], in0=lg[:, c0:c0 + 128], in1=m01, op=ALU.mult),
                                reads=[blg, self.b_cbf], writes=[blg], n=128)
                        psw, bpsw = self.psum(range(6))
                        k.group("pe", [
                            lambda e, psw=psw, lg=lg, c0=c0: e.matmul(psw[:, c0:512], lhsT=uincl, rhs=lg[:, c0:512], start=True, stop=False),
                            lambda e, psw=psw, kt=kt, kbs=kbs, qsl=qsl, c0=c0: e.matmul(
                                psw[:, c0:512], lhsT=kt[:, kbs], rhs=nqT[:, qsl], start=False, stop=True)],
                            reads=[blg, self.b_cbf, bkt, bnqT], writes=[bpsw], n=wn)
                        k.op("dve", lambda e, tmp=tmp, psw=psw, R=R, c0=c0: e.tensor_tensor(
                            out=tmp[:, c0:512], in0=psw[:, c0:512], in1=R[:, c0:512], op=ALU.add), reads=[bpsw, bR], writes=[btmp], n=wn)
                        k.op("act", lambda e, tmp=tmp, pT=pT, c0=c0: e.activation(
                            out=pT[:, c0:512], in_=tmp[:, c0:512], func=AF.Exp, scale=-1.0), reads=[btmp], writes=[bpT], n=wn)
                        if diag:
                            k.op("dve", lambda e, pT=pT, c0=c0: e.tensor_tensor(
                                out=pT[:, c0:c0 + 128], in0=pT[:, c0:c0 + 128], in1=m01, op=ALU.mult),
                                reads=[bpT, self.b_cbf], writes=[bpT], n=128)
                        if kb > 0:
                            psc, bpsc = self.psum(range(6))
                            k.op("pe", lambda e, psc=psc, lg=lg, c0=c0: e.matmul(
                                psc[:, c0:512], lhsT=ones, rhs=lg[:, c0:512], start=True, stop=True),
                                reads=[blg, self.b_cbf], writes=[bpsc], n=wn)
                            k.op("dve", lambda e, psc=psc, R=R, c0=c0: e.tensor_tensor(
                                out=R[:, c0:512], in0=R[:, c0:512], in1=psc[:, c0:512], op=ALU.add), reads=[bpsc, bR], writes=[bR], n=wn)
                        is_first = first[0]; first[0] = False
                        is_last = (kb == 0 and h == 1)
                        k.op("pe", lambda e, pT=pT, kb=kb, h=h, c0=c0, is_first=is_first, is_last=is_last: e.matmul(
                            po[:, c0:512], lhsT=vtp[:, kb, h, :], rhs=pT[:, c0:512], start=is_first, stop=is_last,
                            skip_group_check=True), reads=[bpT, bvtp], writes=[bpo], n=wn)
                k.op("act", lambda e, qr=qr: e.activation(
                    out=self.yT[:, slot * 4 + p, qr * 512:(qr + 1) * 512], in_=po[:, :], func=AF.Copy),
                    reads=[bpo], writes=[self.b_yT[slot * 4 + p][qr]])
        loads = [
            (lambda w, i=i: w[:].rearrange("p (a b) -> p a b", a=8)[:, :, i * 128:(i + 1) * 128],
             self.w_in[l][:, off + p * 128:off + (p + 1) * 128].rearrange("(kc kp) n -> kp kc n", kp=128))
            for i, off in enumerate((O_DQ, O_DK, O_DV))]
        return (loads, fn)

    for p in range(self.flags.get("d_pairs", 4)):
        steps.append(pair_step(p))

    def end(w, bw):
        self.arena_close()
    steps.append(([], end))


Prog.mix_d = _mix_d


def _mix_a(self, l, steps, slot):
    k = self.k
    st = {}
    ROT = range(5)

    def begin(w, bw):
        self.arena_open()
        st["rT"] = self.at("a_rT", [16, SEQ], BF16)
        st["wgk"] = self.at("a_wgk", [16, 256], BF16)
        st["bgk"] = self.at("a_bgk", [128, 256], F32)
        st["ng"] = self.at("a_ng", [128, 128], F32)
        st["sg"] = self.at("a_sg", [128, 16, 256], BF16)
        st["qT"] = self.at("a_qT", [128, SEQ], BF16)
        st["kd"] = self.at("a_kd", [128, 16, 128], BF16)
        st["vtm"] = self.at("a_vtm", [128, 16, 256], BF16)
        st["xg"] = [self.at(f"a_xg{i}", [128, 128], F32) for i in range(2)]
        st["gkn"] = [self.at(f"a_gkn{i}", [128, 128], BF16) for i in range(2)]
        st["edec"] = [self.at(f"a_edec{i}", [128, 128], F32) for i in range(2)]
        st["etot"] = self.at("a_etot", [128, 32], F32)
        st["state"] = [self.at(f"a_state{i}", [128, 128], F32) for i in range(2)]
        st["sbf"] = [self.at(f"a_sbf{i}", [128, 256], BF16) for i in range(2)]
        st["ss"] = [self.at(f"a_ss{i}", [128, 2], F32) for i in range(2)]
        st["junk"] = self.at("a_junk", [128, 128], F32)
        st["tf"] = [self.at(f"a_tf{i}", [128, 256], F32) for i in range(2)]
        st["ytm"] = [self.at(f"a_ytm{i}", [128, 256], BF16) for i in range(2)]
        self.dma("pool", st["wgk"][0][:], self.a_w_gk[l], writes=[st["wgk"][1]])
        self.dma("sp", st["bgk"][0][:], self.a_bgk[l], writes=[st["bgk"][1]])
        self.dma("sp", st["ng"][0][:], self.a_ng[l], writes=[st["ng"][1]])
    steps.append(([], begin))

    def step0(p):
        def fn(w, bw):
            k.sched_begin()
            fn_(w, bw)
            k.sched_end(("a0", p))

        def fn_(w, bw):
            wv = w[:].rearrange("p (a b) -> p a b", a=8)
            rT, brT = st["rT"]; sg, bsg = st["sg"]
            if p == 0:
                for tt in range(4):
                    ts = slice(tt * 512, (tt + 1) * 512)
                    ps, bps = self.psum(ROT)
                    k.group("pe", [lambda e, kc=kc, ps=ps, ts=ts: e.matmul(
                        ps[0:16, :], lhsT=wv[:, kc, 256:272], rhs=self.xnT[:, kc, ts], start=(kc == 0), stop=(kc == 7))
                        for kc in range(8)], reads=[bw, self.b_xn[tt]], writes=[bps])
                    k.op("act", lambda e, ps=ps, ts=ts: e.activation(out=rT[0:16, ts], in_=ps[0:16, :], func=AF.Copy),
                         reads=[bps], writes=[brT])
            for t2 in range(8):
                ps, bps = self.psum(ROT)
                fns = []
                for j in range(2):
                    tb = t2 * 2 + j
                    for kc in range(8):
                        fns.append(lambda e, kc=kc, ps=ps, j=j, tb=tb: e.matmul(
                            ps[:, j * 256:(j + 1) * 256], lhsT=self.xnT[:, kc, tb * 128:(tb + 1) * 128], rhs=wv[:, kc, 0:256],
                            start=(kc == 0), stop=(kc == 7)))
                k.group("pe", fns, reads=[bw, self.b_xn[t2 // 2]], writes=[bps])
                k.op("act", lambda e, ps=ps, t2=t2: e.activation(
                    out=sg[:, t2 * 2:t2 * 2 + 2, :], in_=ps[:].rearrange("p (a b) -> p a b", a=2), func=AF.Silu),
                    reads=[bps], writes=[bsg])
        loads = [
            (lambda w: w[:].rearrange("p (a b) -> p a b", a=8)[:, :, 0:256],
             self.w_in[l][:, O_AG + p * 256:O_AG + (p + 1) * 256].rearrange("(kc kp) n -> kp kc n", kp=128)),
            (lambda w: w[:].rearrange("p (a b) -> p a b", a=8)[:, :, 256:272],
             self.w_in[l][:, O_AR:O_AR + 16].rearrange("(kc kp) n -> kp kc n", kp=128)),
        ]
        return (loads, fn)

    def step1(p):
        def fn(w, bw):
            k.sched_begin()
            fn_(w, bw)
            k.sched_end(("a1", p))

        def fn_(w, bw):
            wv = w[:].rearrange("p (a b) -> p a b", a=8)
            rT, brT = st["rT"]; sg, bsg = st["sg"]; wgk, bwgk = st["wgk"]; bgk, bbgk = st["bgk"]; ng, bng = st["ng"]
            qT, bqT = st["qT"]; kd, bkd = st["kd"]; vtm, bvtm = st["vtm"]
            etot, betot = st["etot"]; junk, bjunk = st["junk"]
            su = self.cbf[:, C_SU:C_SU + 128]
            ch = self.cbf[:, C_CH:C_CH + 2]
            ident = self.cbf[:, C_IDENT:C_IDENT + 128]
            ptot, bptot = self.ps[7], self.b_ps[7]
            for tt in range(4):
                ts = slice(tt * 512, (tt + 1) * 512)
                ps, bps = self.psum(ROT)
                k.group("pe", [lambda e, kc=kc, ps=ps, ts=ts: e.matmul(
                    ps[:], lhsT=wv[:, kc, 0:128], rhs=self.xnT[:, kc, ts], start=(kc == 0), stop=(kc == 7)) for kc in range(8)],
                    reads=[bw, self.b_xn[tt]], writes=[bps])
                k.op("act", lambda e, ps=ps, ts=ts: e.activation(out=qT[:, ts], in_=ps[:], func=AF.Identity, scale=0.125),
                     reads=[bps], writes=[bqT])
            for tb in range(16):
                tbs = slice(tb * 128, (tb + 1) * 128)
                xg, bxg = st["xg"][tb % 2]; gkn, bgkn = st["gkn"][tb % 2]; edec, bedec = st["edec"][tb % 2]
                pg, bpg = self.psum(ROT)
                k.op("pe", lambda e, pg=pg, tbs=tbs: e.matmul(
                    pg[:, 0:128], lhsT=rT[0:16, tbs], rhs=wgk[0:16, p * 128:(p + 1) * 128], start=True, stop=True),
                    reads=[brT, bwgk], writes=[bpg])
                k.op("dve", lambda e, pg=pg, xg=xg: e.tensor_tensor(
                    out=xg[:], in0=pg[:, 0:128], in1=bgk[:, p * 128:(p + 1) * 128], op=ALU.add), reads=[bpg, bbgk], writes=[bxg])
                k.op("act", lambda e, xg=xg: e.activation(out=xg[:], in_=xg[:], func=AF.Exp, scale=-1.0), reads=[bxg], writes=[bxg])
                k.op("act", lambda e, xg=xg, gkn=gkn: e.activation(out=gkn[:], in_=xg[:], func=AF.Ln, bias=1.0),
                     reads=[bxg], writes=[bgkn])
                pd, bpd = self.psum(ROT)
                k.op("pe", lambda e, pd=pd, gkn=gkn: e.matmul(pd[:, 0:128], lhsT=su, rhs=gkn[:], start=True, stop=True),
                     reads=[bgkn, self.b_cbf], writes=[bpd])
                k.op("act", lambda e, pd=pd, edec=edec: e.activation(out=edec[:], in_=pd[:, 0:128], func=AF.Exp, scale=-1.0 / 16.0),
                     reads=[bpd], writes=[bedec])
                k.op("pe", lambda e, gkn=gkn, tb=tb: e.matmul(ptot[:, 2 * tb:2 * tb + 2], lhsT=gkn[:], rhs=ch, start=True, stop=True),
                     reads=[bgkn, self.b_cbf], writes=[bptot])
                pkv, bpkv = self.psum(ROT)
                k.group("pe", [lambda e, kc=kc, pkv=pkv, tbs=tbs: e.matmul(
                    pkv[:, 0:384], lhsT=self.xnT[:, kc, tbs], rhs=wv[:, kc, 128:512], start=(kc == 0), stop=(kc == 7))
                    for kc in range(8)], reads=[bw, self.b_xn[tb // 4]], writes=[bpkv])
                k.op("dve", lambda e, pkv=pkv, edec=edec, tb=tb: e.tensor_tensor(
                    out=kd[:, tb, :], in0=pkv[:, 0:128], in1=edec[:], op=ALU.mult), reads=[bpkv, bedec], writes=[bkd])
                k.op("dve", lambda e, pkv=pkv, tb=tb: e.tensor_copy(out=vtm[:, tb, :], in_=pkv[:, 128:384]),
                     reads=[bpkv], writes=[bvtm])
            k.op("act", lambda e: e.activation(out=etot[:], in_=ptot[:, 0:32], func=AF.Exp, scale=-1.0 / 16.0),
                 reads=[bptot], writes=[betot])
            k.op("dve", lambda e: e.memset(st["state"][1][0][:], 0.0), writes=[st["state"][1][1]], n=128)
            for i in range(2):
                k.op("pool", lambda e, i=i: e.memset(st["sbf"][i][0][:], 0.0), writes=[st["sbf"][i][1]])

            S = {}

            def s0(c):
                tb, half = c // 2, c % 2
                rows = slice(half * 64, (half + 1) * 64)
                pk, bpk = self.psum(ROT)
                k.group("pe", [lambda e, hh=hh, pk=pk, tb=tb, rows=rows: e.matmul(
                    pk[hh * 64:(hh + 1) * 64, 0:128], lhsT=kd[rows, tb, hh * 64:(hh + 1) * 64], rhs=vtm[rows, tb, hh * 128:(hh + 1) * 128],
                    start=True, stop=True, skip_group_check=True) for hh in range(2)], reads=[bkd, bvtm], writes=[bpk])
                S[c] = (pk, bpk)

            def s1(c):
                pk, bpk = S[c]
                sbf, bsbf = st["sbf"][c % 2]
                sprev, bsprev = st["state"][(c + 1) % 2]
                state, bstate = st["state"][c % 2]
                k.op("dve", lambda e, pk=pk, c=c, sprev=sprev, state=state: e.scalar_tensor_tensor(
                    out=state[:], in0=sprev[:], scalar=etot[:, c:c + 1], in1=pk[:, 0:128], op0=ALU.mult, op1=ALU.add),
                    reads=[bpk, betot, bsprev], writes=[bstate], n=128)
                for hh in range(2):
                    eng = "act" if hh == 0 else "pool"
                    if eng == "act":
                        k.op("act", lambda e, sbf=sbf, hh=hh, state=state: e.activation(
                            out=sbf[hh * 64:(hh + 1) * 64, hh * 128:(hh + 1) * 128], in_=state[hh * 64:(hh + 1) * 64, :], func=AF.Copy),
                            reads=[bstate], writes=[bsbf], n=128)
                    else:
                        k.op("pool", lambda e, sbf=sbf, hh=hh, state=state: e.tensor_copy(
                            out=sbf[hh * 64:(hh + 1) * 64, hh * 128:(hh + 1) * 128], in_=state[hh * 64:(hh + 1) * 64, :]),
                            reads=[bstate], writes=[bsbf], n=128)
                S[c] = (sbf, bsbf)

            def s2(c):
                sbf, bsbf = S[c]
                tb, half = c // 2, c % 2
                rows = slice(half * 64, (half + 1) * 64)
                po, bpo = self.ps[5 + tb % 2], self.b_ps[5 + tb % 2]
                k.op("pe", lambda e, po=po, rows=rows, c=c, sbf=sbf: e.matmul(
                    po[rows, 0:256], lhsT=qT[:, c * 64:(c + 1) * 64], rhs=sbf[:, 0:256],
                    start=True, stop=True, skip_group_check=True), reads=[bqT, bsbf], writes=[bpo])
                if half == 1:
                    ss, bss = st["ss"][tb % 2]; tf, btf = st["tf"][tb % 2]; ytm, bytm = st["ytm"][tb % 2]
                    for hh in range(2):
                        k.op("act", lambda e, hh=hh, po=po, ss=ss: e.activation(
                            out=junk[:], in_=po[:, hh * 128:(hh + 1) * 128], func=AF.Square, accum_out=ss[:, hh:hh + 1]),
                            reads=[bpo], writes=[bjunk, bss])
                    k.op("act", lambda e, ss=ss: e.activation(out=ss[:], in_=ss[:], func=AF.Ln, scale=1.0 / 128.0, bias=EPS),
                         reads=[bss], writes=[bss])
                    k.op("act", lambda e, ss=ss: e.activation(out=ss[:], in_=ss[:], func=AF.Exp, scale=-0.5), reads=[bss], writes=[bss])
                    for hh in range(2):
                        k.op("dve", lambda e, hh=hh, po=po, ss=ss, tf=tf: e.scalar_tensor_tensor(
                            out=tf[:, hh * 128:(hh + 1) * 128], in0=po[:, hh * 128:(hh + 1) * 128], scalar=ss[:, hh:hh + 1], in1=ng[:],
                            op0=ALU.mult, op1=ALU.mult), reads=[bpo, bss, bng], writes=[btf])
                    k.op("pool", lambda e, tf=tf, ytm=ytm, tb=tb: e.tensor_tensor(out=ytm[:], in0=tf[:], in1=sg[:, tb, :], op=ALU.mult),
                         reads=[btf, bsg], writes=[bytm])
                    pt, bpt = self.psum(ROT)
                    ptb = pt[:].bitcast(BF16)
                    k.group("pe", [lambda e, j=j, ytm=ytm, ptb=ptb: e.transpose(
                        out=ptb[:, j * 128:(j + 1) * 128], in_=ytm[:, j * 128:(j + 1) * 128], identity=ident) for j in range(2)],
                        reads=[bytm, self.b_cbf], writes=[bpt])
                    c0 = slot * 4 + p * 2
                    k.op("act", lambda e, ptb=ptb, tb=tb, c0=c0: e.activation(
                        out=self.yT[:, c0:c0 + 2, tb * 128:(tb + 1) * 128], in_=ptb[:, 0:256].rearrange("p (a b) -> p a b", a=2),
                        func=AF.Copy), reads=[bpt], writes=[self.b_yT[c0][tb // 4], self.b_yT[c0 + 1][tb // 4]])

            pipeline(32, [s0, s1, s2])
        loads = [
            (lambda w: w[:].rearrange("p (a b) -> p a b", a=8)[:, :, 0:128],
             self.w_in[l][:, O_AQ + p * 128:O_AQ + (p + 1) * 128].rearrange("(kc kp) n -> kp kc n", kp=128)),
            (lambda w: w[:].rearrange("p (a b) -> p a b", a=8)[:, :, 128:256],
             self.w_in[l][:, O_AK + p * 128:O_AK + (p + 1) * 128].rearrange("(kc kp) n -> kp kc n", kp=128)),
            (lambda w: w[:].rearrange("p (a b) -> p a b", a=8)[:, :, 256:512],
             self.w_in[l][:, O_AV + p * 256:O_AV + (p + 1) * 256].rearrange("(kc kp) n -> kp kc n", kp=128)),
        ]
        return (loads, fn)

    for p in range(2):
        steps.append(step0(p))
        steps.append(step1(p))

    def end(w, bw):
        self.arena_close()
    steps.append(([], end))


Prog.mix_a = _mix_a
```
